# Optimizing a Trainium2 kernel written in Bass

```python
import math
import jax, jax.numpy as jnp
from jax import lax
import numpy as np

D_MODEL = 1024
BATCH = 2
SEQ = 8192
DEPTH = 1

N_MEM = 256
D_LRU = D_MODEL // 2
D_CONF = D_MODEL // 2
D_MIX = D_LRU + D_CONF
LRU_HEADS = 8
LRU_HD = D_LRU // LRU_HEADS
LRU_CONV = 4
RG_C = 8.0
CONF_CONV = 31
CONF_GROUPS = 8
XA_HEADS = 4
XA_HD = D_MODEL // XA_HEADS
D_FF = 3 * D_MODEL
FFN_CONV = 3
EPS = 1e-6

kernel_name = "hybrid_rglru_conformer_xattn_convffn"


def rms_norm(x, g):
    xf = x.astype(jnp.float32)
    y = xf * lax.rsqrt(jnp.mean(xf * xf, axis=-1, keepdims=True) + EPS)
    return y.astype(x.dtype) * g


def layer_norm(x, g, b):
    xf = x.astype(jnp.float32)
    mu = jnp.mean(xf, axis=-1, keepdims=True)
    xc = xf - mu
    var = jnp.mean(xc * xc, axis=-1, keepdims=True)
    return (xc * lax.rsqrt(var + EPS)).astype(x.dtype) * g + b


def causal_dwconv(x, w, b):
    k = w.shape[0]
    y = lax.conv_general_dilated(
        x, w[:, None, :], window_strides=(1,), padding=[(k - 1, 0)],
        dimension_numbers=("NWC", "WIO", "NWC"), feature_group_count=x.shape[-1])
    return y + b


def rg_lru(x, w_a, b_a, w_x, b_x, lam):
    bsz, s, c = x.shape
    xh = x.reshape(bsz, s, LRU_HEADS, LRU_HD)
    r = jax.nn.sigmoid(jnp.einsum("bshi,hij->bshj", xh, w_a).reshape(bsz, s, c) + b_a)
    i = jax.nn.sigmoid(jnp.einsum("bshi,hij->bshj", xh, w_x).reshape(bsz, s, c) + b_x)
    log_a = -RG_C * r.astype(jnp.float32) * jax.nn.softplus(-lam.astype(jnp.float32))
    a = jnp.exp(log_a)
    mult = jnp.sqrt(-jnp.expm1(2.0 * log_a))
    u = mult * (i * x).astype(jnp.float32)

    def combine(left, right):
        a1, b1 = left
        a2, b2 = right
        return a1 * a2, a2 * b1 + b2

    _, h = lax.associative_scan(combine, (a, u), axis=1)
    return h.astype(x.dtype)


def setup_inputs(seed: int = 0) -> dict:
    key = jax.random.key(seed)
    ks = iter(jax.random.split(key, 40))
    f32 = jnp.float32

    def nrm(shape, scale):
        return jax.random.normal(next(ks), shape, f32) * scale

    def gain(shape):
        return 1.0 + 0.01 * jax.random.normal(next(ks), shape, f32)

    L = DEPTH
    x = jax.random.normal(next(ks), (BATCH, SEQ, D_MODEL), f32)
    mem = jax.random.normal(next(ks), (BATCH, N_MEM, D_MODEL), f32)
    a_c = jax.random.uniform(next(ks), (L, D_LRU), f32, 0.9, 0.999)
    sig = a_c ** (1.0 / RG_C)
    lam = jnp.log(sig) - jnp.log1p(-sig)
    return {
        "x": x,
        "mem": mem,
        "mix_norm_g": gain((L, D_MODEL)),
        "w_in": nrm((L, D_MODEL, 2 * D_MIX), D_MODEL ** -0.5),
        "lru_conv_w": nrm((L, LRU_CONV, D_LRU), LRU_CONV ** -0.5),
        "lru_conv_b": nrm((L, D_LRU), 0.01),
        "lru_w_a": nrm((L, LRU_HEADS, LRU_HD, LRU_HD), LRU_HD ** -0.5),
        "lru_b_a": nrm((L, D_LRU), 0.01),
        "lru_w_x": nrm((L, LRU_HEADS, LRU_HD, LRU_HD), LRU_HD ** -0.5),
        "lru_b_x": nrm((L, D_LRU), 0.01),
        "lru_lambda": lam,
        "conf_conv_w": nrm((L, CONF_CONV, D_CONF), CONF_CONV ** -0.5),
        "conf_conv_b": nrm((L, D_CONF), 0.01),
        "conf_ln_g": gain((L, D_CONF)),
        "conf_ln_b": nrm((L, D_CONF), 0.01),
        "w_out": nrm((L, D_MIX, D_MODEL), D_MIX ** -0.5),
        "xa_norm_g": gain((L, D_MODEL)),
        "mem_norm_g": gain((L, D_MODEL)),
        "w_q": nrm((L, D_MODEL, D_MODEL), D_MODEL ** -0.5),
        "w_kv": nrm((L, D_MODEL, 2 * D_MODEL), D_MODEL ** -0.5),
        "w_o": nrm((L, D_MODEL, D_MODEL), D_MODEL ** -0.5),
        "ffn_norm_g": gain((L, D_MODEL)),
        "w_up": nrm((L, D_MODEL, 2 * D_FF), D_MODEL ** -0.5),
        "ffn_conv_w": nrm((L, FFN_CONV, D_FF), FFN_CONV ** -0.5),
        "ffn_conv_b": nrm((L, D_FF), 0.01),
        "w_down": nrm((L, D_FF, D_MODEL), D_FF ** -0.5),
        "final_norm_g": gain((D_MODEL,)),
    }


def reference(x, mem, mix_norm_g, w_in, lru_conv_w, lru_conv_b, lru_w_a, lru_b_a,
              lru_w_x, lru_b_x, lru_lambda, conf_conv_w, conf_conv_b, conf_ln_g,
              conf_ln_b, w_out, xa_norm_g, mem_norm_g, w_q, w_kv, w_o, ffn_norm_g,
              w_up, ffn_conv_w, ffn_conv_b, w_down, final_norm_g):
    bsz, s, d = x.shape
    m_len = mem.shape[1]
    for l in range(DEPTH):
        h = rms_norm(x, mix_norm_g[l])
        z = h @ w_in[l]
        lru_x, lru_gate, conf_a, conf_b = jnp.split(
            z, [D_LRU, 2 * D_LRU, 2 * D_LRU + D_CONF], axis=-1)
        lru_x = causal_dwconv(lru_x, lru_conv_w[l], lru_conv_b[l])
        y_lru = rg_lru(lru_x, lru_w_a[l], lru_b_a[l], lru_w_x[l], lru_b_x[l],
                       lru_lambda[l]) * jax.nn.gelu(lru_gate, approximate=True)
        c = conf_a * jax.nn.sigmoid(conf_b)
        c = causal_dwconv(c, conf_conv_w[l], conf_conv_b[l])
        c = jax.nn.silu(layer_norm(c, conf_ln_g[l], conf_ln_b[l]))
        y = jnp.concatenate([y_lru, c], axis=-1) @ w_out[l]
        x = x + y

        h = rms_norm(x, xa_norm_g[l])
        m = rms_norm(mem, mem_norm_g[l])
        q = (h @ w_q[l]).reshape(bsz, s, XA_HEADS, XA_HD)
        kv = m @ w_kv[l]
        k, v = jnp.split(kv, 2, axis=-1)
        k = k.reshape(bsz, m_len, XA_HEADS, XA_HD)
        v = v.reshape(bsz, m_len, XA_HEADS, XA_HD)
        scores = jnp.einsum("bshd,bmhd->bhsm", q, k).astype(jnp.float32) * (XA_HD ** -0.5)
        p = jax.nn.softmax(scores, axis=-1).astype(v.dtype)
        o = jnp.einsum("bhsm,bmhd->bshd", p, v).reshape(bsz, s, d)
        x = x + o @ w_o[l]

        h = rms_norm(x, ffn_norm_g[l])
        gu = h @ w_up[l]
        g, u = jnp.split(gu, 2, axis=-1)
        g = causal_dwconv(g, ffn_conv_w[l], ffn_conv_b[l])
        x = x + (jax.nn.gelu(g, approximate=True) * u) @ w_down[l]
    return rms_norm(x, final_norm_g)
```

```python
import numpy as np
from contextlib import ExitStack
import concourse.bass as bass
import concourse.mybir as mybir
from concourse.bass_utils import run_bass_kernel_spmd

F32 = mybir.dt.float32
BF16 = mybir.dt.bfloat16
I32 = mybir.dt.int32
AF = mybir.ActivationFunctionType
ALU = mybir.AluOpType

D = 1024
SEQ = 8192
TOWN = 2048
HALO = 32
NROWS = 8192
EPS = 1e-6
GK = 0.7978845608028654

PCOLS = {}
_o = 0
for _n, _w in (("g1", 8), ("lcw", 16), ("lcb", 4), ("ba", 4), ("bx", 4), ("lam", 4), ("ccw", 124),
               ("ccb", 4), ("lng", 4), ("lnb", 4), ("g2", 8), ("gm", 8), ("g3", 8), ("fcw", 72),
               ("fcb", 24), ("mask", 17)):
    PCOLS[_n] = _o
    _o += _w
NPV = _o

DEBUG_STAGE = None


class KB:
    NS = {"sp": 8, "pool": 16}
    SLACK = 0.0
    ENG = ("pe", "act", "dve", "pool", "sp")

    def __init__(self, nc, es):
        self.nc = nc
        self.sem = {n: es.enter_context(nc.semaphore("s_" + n)) for n in self.ENG}
        self.dsem = {q: [es.enter_context(nc.semaphore("d_%s%d" % (q, i))) for i in range(self.NS[q])]
                     for q in ("sp", "pool")}
        self.ops = []
        self.lastw = {}
        self.readers = {}
        self.phase = 0
        self.psn = 0
        self.streams = None
        self._dops = {"sp": [], "pool": []}
        self.prole = {}
        self.rr_only = False
        self.W = 512

    def _rec(self, eng, fn, reads, writes, cost, kind, tag=None, nbytes=0):
        preds = set()
        for k in reads:
            t = self.lastw.get(k)
            if t is not None:
                preds.add(t)
        for k in writes:
            t = self.lastw.get(k)
            if t is not None:
                preds.add(t)
            preds.update(self.readers.get(k, ()))
        i = len(self.ops)
        self.ops.append(dict(eng=eng, fn=fn, preds=preds, cost=float(cost), kind=kind, tag=tag, nbytes=nbytes,
                             phase=self.phase))
        for k in writes:
            self.lastw[k] = i
            self.readers[k] = set()
        for k in reads:
            self.readers.setdefault(k, set()).add(i)
        return i

    def op(self, eng, fn, reads=(), writes=(), cost=600.0, tag=None):
        return self._rec(eng, fn, reads, writes, cost, "op", tag)

    def dma(self, q, fn, reads=(), writes=(), nbytes=65536):
        return self._rec(q, fn, reads, writes, 0.0, "dma", None, nbytes)

    def barrier(self):
        self.phase += 1

    POOLS = {"FG": (0, 1, 2), "FU": (3, 4), "FD": (5, 6, 7), "T": (0, 1), "Z": (2, 3), "C": (4, 5), "G0": (6,), "G1": (7,),
             "A": (0, 1, 2, 3), "M": (4, 5), "O": (6, 7)}

    def psum(self, role=None):
        if role is None or self.rr_only:
            b = self.psn
            self.psn = (self.psn + 1) % 8
            return b
        pool = self.POOLS[role]
        k = self.prole.get(role, 0)
        self.prole[role] = k + 1
        return pool[k % len(pool)]

    def finalize(self, window=56):
        ops = self.ops
        nph = self.phase + 1
        finish = [None] * len(ops)
        pos = [None] * len(ops)
        streams = {n: [] for n in self.ENG}
        eng_free = {n: 0.0 for n in self.ENG}
        cnt = {n: 0 for n in self.ENG}
        dn = {"sp": 0, "pool": 0}
        dhist = {"sp": [], "pool": []}
        seen = {n: {} for n in self.ENG}
        act_tag = [None]
        pipe_free = [0.0]
        BW = 170.0
        tnow = 0.0
        succ = [[] for _ in ops]
        for i, o in enumerate(ops):
            for p in o["preds"]:
                if ops[p]["phase"] == o["phase"]:
                    succ[p].append(i)
        blevel = [0.0] * len(ops)
        for i in range(len(ops) - 1, -1, -1):
            o = ops[i]
            c = o["cost"] if o["kind"] != "dma" else (2500.0 + o["nbytes"] / BW)
            m = 0.0
            for q in succ[i]:
                if blevel[q] > m:
                    m = blevel[q]
            blevel[i] = c + 200.0 + m
        for ph in range(nph):
            pend = {n: [] for n in self.ENG}
            for i, o in enumerate(ops):
                if o["phase"] == ph:
                    pend[o["eng"]].append(i)
            nleft = sum(len(v) for v in pend.values())
            t_phase = max(eng_free.values()) if ph > 0 else 0.0
            if ph > 0:
                fin_prev = [finish[i] for i, o in enumerate(ops) if o["phase"] == ph - 1 and o["kind"] != "dma"]
                if fin_prev:
                    t_phase = max(t_phase, max(fin_prev))
                for n in self.ENG:
                    waits = []
                    for m in self.ENG:
                        if m != n and cnt[m] > seen[n].get("s_" + m, 0):
                            waits.append((self.sem[m], cnt[m]))
                            seen[n]["s_" + m] = cnt[m]
                    if waits:
                        streams[n].append((waits, None, None, 0))
                    eng_free[n] = max(eng_free[n], t_phase)
            while nleft > 0:
                best = None
                for n in self.ENG:
                    pl = pend[n]
                    if not pl:
                        continue
                    cb = None
                    for i in pl[:window]:
                        o = ops[i]
                        ready = 0.0
                        ok = True
                        for p in o["preds"]:
                            f = finish[p]
                            if f is None:
                                ok = False
                                break
                            lat = 260.0 if ops[p]["eng"] != n else 110.0
                            if ops[p]["eng"] == n and n == "pe":
                                lat = 0.0
                                f = f - ops[p]["cost"]
                            if f + lat > ready:
                                ready = f + lat
                        if not ok:
                            continue
                        st = max(eng_free[n], ready)
                        if o["kind"] == "dma":
                            hist = dhist[n]
                            if len(hist) >= self.NS[n]:
                                pi = self._dops[n][len(hist) - self.NS[n]]
                                st = max(st, finish[pi])
                        if n == "act" and o["tag"] is not None and o["tag"] != act_tag[0]:
                            st += 2700.0
                        if st <= eng_free[n] + self.SLACK:
                            key = (0, -blevel[i], st)
                        else:
                            key = (1, st, -blevel[i])
                        if cb is None or key < cb[2]:
                            cb = (st, i, key)
                    if cb is not None and (best is None or cb[0] < best[0] - 1e-9):
                        best = (cb[0], cb[1], n)
                assert best is not None, "scheduler deadlock"
                st, i, n = best
                o = ops[i]
                pend[n].remove(i)
                nleft -= 1
                need = {}
                for p in o["preds"]:
                    if ops[p]["eng"] == n and n == "pe":
                        continue
                    sem, val, name = pos[p]
                    if seen[n].get(name, 0) >= val:
                        continue
                    if name not in need or need[name][1] < val:
                        need[name] = (sem, val)
                if o["kind"] == "dma":
                    k = dn[n]
                    slot = k % self.NS[n]
                    val = (k // self.NS[n] + 1) * 16
                    name = "d_%s%d" % (n, slot)
                    if val > 16 and seen[n].get(name, 0) < val - 16:
                        need[name] = (self.dsem[n][slot], val - 16)
                    dn[n] += 1
                    dhist[n].append((slot, val))
                    self._dops[n].append(i)
                    pos[i] = (self.dsem[n][slot], val, name)
                    issue = 100.0 if n == "sp" else 620.0
                    tdat = o["nbytes"] / (BW if n == "sp" else 220.0)
                    pstart = max(st + issue, pipe_free[0])
                    pipe_free[0] = pstart + tdat
                    finish[i] = pstart + tdat + 1800.0
                    eng_free[n] = st + issue
                    inc = 16
                    semo = self.dsem[n][slot]
                else:
                    cnt[n] += 1
                    pos[i] = (self.sem[n], cnt[n], "s_" + n)
                    finish[i] = st + o["cost"]
                    eng_free[n] = finish[i]
                    if n == "act" and o["tag"] is not None:
                        act_tag[0] = o["tag"]
                    inc = 1
                    semo = self.sem[n]
                for name, (sem, val) in need.items():
                    seen[n][name] = val
                streams[n].append((list(need.values()), o["fn"], semo, inc))
        waits = []
        for q in ("sp", "pool"):
            for j in range(self.NS[q]):
                done = [v for (slot, v) in dhist[q] if slot == j]
                if done:
                    waits.append((self.dsem[q][j], done[-1]))
        streams["sp"].append((waits, None, None, 0))
        self.streams = streams
        self.sim_time = max(eng_free.values())

    _dops = None

    def replay(self, h, name):
        for waits, fn, sem, inc in self.streams[name]:
            for s, v in waits:
                h.wait_ge(s, v)
            if fn is not None:
                ins = fn(h)
                ins.then_inc(sem, inc)


def build_nc():
    nc = bass.Bass("TRN2", target_bir_lowering=False)
    dt = lambda n, s, kind="ExternalInput": nc.dram_tensor(n, s, F32, kind=kind).ap()
    xall = dt("xall", [NROWS, D])
    memd = dt("mem", [256, D])
    pvd = dt("pv", [128, NPV])
    gfd = dt("gfin", [128, D])
    bdd = dt("bd", [128, 4 * 2 * 128])
    lcbd = dt("lcbrow", [1, 512])
    w_in = dt("w_in", [D, 2048])
    w_out = dt("w_out", [D, D])
    w_q = dt("w_q", [D, D])
    w_kv = dt("w_kv", [D, 2048])
    w_o = dt("w_o", [D, D])
    w_up = dt("w_up", [D, 6144])
    w_down = dt("w_down", [3072, D])
    yd = dt("y", [TOWN, D], kind="ExternalOutput")

    with ExitStack() as es:
        kb = KB(nc, es)
        sb = lambda n, s, d=F32, st=None: (st or es).enter_context(nc.sbuf_tensor(n, s, d))
        PS = [es.enter_context(nc.psum_tensor("ps%d" % i, [128, 512], F32)) for i in range(8)]
        PSB = [p.bitcast(BF16) for p in PS]

        X = sb("X", [128, 17, D])
        WS0 = sb("WS0", [128, 16384], BF16)
        WS1 = sb("WS1", [128, 8192], BF16)
        WS2 = sb("WS2", [128, 8448], BF16)
        pv = sb("pvs", [128, NPV])
        ident = sb("ident", [128, 128], BF16)
        identf = sb("identf", [128, 128])
        iot = sb("iot", [128, 128], I32)
        ones_s = sb("ones_s", [128, 128], BF16)
        ones1 = sb("ones1", [128, 128], BF16)
        neghT = sb("neghT", [128, 16])
        sm = sb("sm", [128, 96])
        BD = sb("BD", [128, 1024], BF16)
        ss = sb("ss", [128, 8])
        rs = sb("rs", [128, 8])
        junk = sb("junk", [128, D], BF16)
        xsb = [sb("xsb%d" % i, [128, D], BF16) for i in range(4)]

        P = lambda name, i=0: pv[:, PCOLS[name] + i:PCOLS[name] + i + 1]
        S_CLA, S_HCLA, S_HBA, S_HBX, S_HM2, S_NHM2, S_TMP = 0, 4, 8, 12, 16, 33, 50

        def SM(c, i=0):
            return sm[:, c + i:c + i + 1]

        pe = lambda fn, r=(), w=(), cost=300.0: kb.op("pe", fn, r, w, cost)
        act = lambda fn, r=(), w=(), cost=None, tag=None: kb.op(
            "act", fn, r, w, (260.0 + 0.85 * kb.W) if cost is None else cost, tag)
        dve = lambda fn, r=(), w=(), cost=None: kb.op("dve", fn, r, w, (110.0 + 1.15 * kb.W) if cost is None else cost)
        pool = lambda fn, r=(), w=(), cost=None: kb.op("pool", fn, r, w,
                                                      (150.0 + 3.4 * kb.W) if cost is None else cost)
        TINY = 120.0

        kb.dma("sp", lambda h: h.dma_start(out=pv[:], in_=pvd), (), ("pv",), nbytes=128 * NPV * 4)
        kb.dma("pool", lambda h: h.dma_start(out=BD[:], in_=bdd), (), ("BD",), nbytes=128 * 1024 * 4)
        pool(lambda h: h.iota(iot[:], [[1, 128]], base=0, channel_multiplier=-1), (), ("iot",))
        pool(lambda h: h.memset(ones_s[:], 1.0 / 512.0), (), ("ones_s",))
        pool(lambda h: h.memset(ones1[:], 1.0), (), ("ones1",))
        pool(lambda h: h.memset(neghT[:], -0.5), (), ("neghT",))
        pool(lambda h: h.memset(ss[:], 1.0), (), tuple("ss%d" % i for i in range(8)))
        pool(lambda h: h.memset(rs[:], 1.0), (), ("rs",))
        dve(lambda h: h.tensor_scalar(out=ident[:], in0=iot[:], scalar1=0.0, scalar2=None, op0=ALU.is_equal),
            ("iot",), ("ident",))
        dve(lambda h: h.tensor_scalar(out=identf[:], in0=iot[:], scalar1=0.0, scalar2=None, op0=ALU.is_equal),
            ("iot",), ("identf",))

        WIN = WS0[:, :].rearrange("p (k n) -> p k n", k=8)
        WOUT = WS1[:, :].rearrange("p (k n) -> p k n", k=8)

        def load_w(dst3, src, ncols, col0, keyname, srccol0=0, nk=8, row0=0):
            for c in range(0, ncols, 1024):
                w = min(1024, ncols - c)
                for k0 in range(0, nk, 4):
                    k1 = min(nk, k0 + 4)
                    kb.dma("pool",
                           (lambda h, k0=k0, k1=k1, c=c, w=w: h.dma_start(
                               out=dst3[:, k0:k1, col0 + c:col0 + c + w],
                               in_=src[row0 + k0 * 128:row0 + k1 * 128, srccol0 + c:srccol0 + c + w].rearrange(
                                   "(k p) n -> p k n", p=128))),
                           (), (keyname,), nbytes=128 * w * 4 * (k1 - k0))

        load_w(WIN, w_in, 512, 0, "WS0a")

        T = lambda i: sm[:, S_TMP + 4 * i:S_TMP + 4 * i + 4]
        lam = pv[:, PCOLS["lam"]:PCOLS["lam"] + 4]
        k_sm = "sm"
        dve(lambda h: h.tensor_scalar(out=T(0), in0=lam, scalar1=-1.0, scalar2=0.0, op0=ALU.mult, op1=ALU.max),
            ("pv",), (k_sm,))
        dve(lambda h: h.tensor_scalar(out=T(1), in0=lam, scalar1=-1.0, scalar2=None, op0=ALU.mult), ("pv",), (k_sm,))
        dve(lambda h: h.tensor_tensor(out=T(1), in0=T(1), in1=lam, op=ALU.max), ("pv", k_sm), (k_sm,))
        act(lambda h: h.activation(out=T(1), in_=T(1), func=AF.Exp, scale=-1.0), (k_sm,), (k_sm,), tag="s0")
        dve(lambda h: h.tensor_scalar(out=T(2), in0=T(1), scalar1=2.0, scalar2=None, op0=ALU.add), (k_sm,), (k_sm,))
        dve(lambda h: h.reciprocal(out=T(2), in_=T(2)), (k_sm,), (k_sm,))
        dve(lambda h: h.tensor_tensor(out=T(1), in0=T(1), in1=T(2), op=ALU.mult), (k_sm,), (k_sm,))
        dve(lambda h: h.tensor_tensor(out=T(2), in0=T(1), in1=T(1), op=ALU.mult), (k_sm,), (k_sm,))
        dve(lambda h: h.tensor_scalar(out=T(3), in0=T(2), scalar1=1.0 / 11, scalar2=1.0 / 9, op0=ALU.mult,
                                      op1=ALU.add), (k_sm,), (k_sm,))
        for cc in (1.0 / 7, 1.0 / 5, 1.0 / 3, 1.0):
            dve(lambda h: h.tensor_tensor(out=T(3), in0=T(3), in1=T(2), op=ALU.mult), (k_sm,), (k_sm,))
            dve(lambda h, cc=cc: h.tensor_scalar(out=T(3), in0=T(3), scalar1=cc, scalar2=None, op0=ALU.add),
                (k_sm,), (k_sm,))
        dve(lambda h: h.tensor_tensor(out=T(3), in0=T(3), in1=T(1), op=ALU.mult), (k_sm,), (k_sm,))
        dve(lambda h: h.scalar_tensor_tensor(out=T(3), in0=T(3), scalar=2.0, in1=T(0), op0=ALU.mult, op1=ALU.add),
            (k_sm,), (k_sm,))
        dve(lambda h: h.tensor_scalar(out=sm[:, S_CLA:S_CLA + 4], in0=T(3), scalar1=-8.0, scalar2=None,
                                      op0=ALU.mult), (k_sm,), (k_sm,))
        dve(lambda h: h.tensor_scalar(out=sm[:, S_HCLA:S_HCLA + 4], in0=T(3), scalar1=-4.0, scalar2=None,
                                      op0=ALU.mult), (k_sm,), (k_sm,))
        dve(lambda h: h.tensor_scalar(out=sm[:, S_HBA:S_HBA + 4], in0=pv[:, PCOLS["ba"]:PCOLS["ba"] + 4],
                                      scalar1=0.5, scalar2=None, op0=ALU.mult), ("pv",), (k_sm,))
        dve(lambda h: h.tensor_scalar(out=sm[:, S_HBX:S_HBX + 4], in0=pv[:, PCOLS["bx"]:PCOLS["bx"] + 4],
                                      scalar1=0.5, scalar2=None, op0=ALU.mult), ("pv",), (k_sm,))
        mk = pv[:, PCOLS["mask"]:PCOLS["mask"] + 17]
        dve(lambda h: h.tensor_scalar(out=sm[:, S_HM2:S_HM2 + 17], in0=mk, scalar1=0.25, scalar2=None,
                                      op0=ALU.mult), ("pv",), (k_sm,))
        dve(lambda h: h.tensor_scalar(out=sm[:, S_NHM2:S_NHM2 + 17], in0=mk, scalar1=-0.25, scalar2=None,
                                      op0=ALU.mult), ("pv",), (k_sm,))

        evac_rr = [0]

        def evac(out_ap, in_ap, reads, writes, scale=None, eng=None):
            if eng is None:
                eng = ("act", "dve")[evac_rr[0] % 2]
                evac_rr[0] += 1
            if eng == "act":
                if scale is None:
                    act(lambda h: h.activation(out=out_ap, in_=in_ap, func=AF.Copy), reads, writes)
                else:
                    act(lambda h: h.activation(out=out_ap, in_=in_ap, func=AF.Copy, scale=scale), reads, writes)
            else:
                if scale is None:
                    dve(lambda h: h.tensor_copy(out=out_ap, in_=in_ap), reads, writes)
                else:
                    dve(lambda h: h.tensor_scalar(out=out_ap, in0=in_ap, scalar1=scale, scalar2=None,
                                                  op0=ALU.mult), reads, writes, cost=70.0 + 0.75 * kb.W)

        def ttiles(W):
            out = []
            r = 0
            while r < W:
                n = min(128, W - r)
                out.append((r, n))
                r += n
            return out

        def norm_T(src_tiles, W, gname, hT, hTkey, col0=0, cast2="act", evac_eng=None, sq_dve=False):
            nt = len(src_tiles)
            kb.W = W
            for i, (xap, xkey, n) in enumerate(src_tiles):
                if sq_dve and i == 3:
                    dve(lambda h, xap=xap, n=n, i=i: h.scalar_tensor_tensor(
                        out=junk2[0:n, :], in0=xap, scalar=1.0, in1=xap, op0=ALU.mult, op1=ALU.mult,
                        accum_out=ss[0:n, i:i + 1]), (xkey,), ("ss%d" % i, "junk2"), cost=1250.0)
                    continue
                act(lambda h, xap=xap, n=n, i=i: h.activation(out=junk[0:n, :], in_=xap, func=AF.Square,
                                                              accum_out=ss[0:n, i:i + 1]),
                    (xkey,), ("ss%d" % i, "junk"), cost=1170.0)
            sskeys = tuple("ss%d" % i for i in range(nt))
            pool(lambda h: h.tensor_scalar(out=rs[:, 0:nt], in0=ss[:, 0:nt], scalar1=1.0 / D, scalar2=EPS,
                                           op0=ALU.mult, op1=ALU.add), sskeys, ("rs",), cost=200.0)
            pool(lambda h: h.tensor_tensor(out=rs[:, 0:nt], in0=rs[:, 0:nt], in1=neghT[:, 0:nt], op=ALU.pow),
                 ("rs", "neghT"), ("rs",), cost=200.0 + 170.0 * nt)
            r = 0
            offs = []
            for i, (xap, xkey, n) in enumerate(src_tiles):
                xb = xsb[i % 4]
                xbk = "xsb%d" % (i % 4)
                offs.append((r, n))
                r += n
                if i % 2 == 0:
                    dve(lambda h, xb=xb, xap=xap, n=n, i=i: h.tensor_scalar(
                        out=xb[0:n, :], in0=xap, scalar1=rs[0:n, i:i + 1], scalar2=None, op0=ALU.mult),
                        (xkey, "rs"), (xbk,), cost=1125.0)
                elif cast2 == "dve":
                    dve(lambda h, xb=xb, xap=xap, n=n, i=i: h.tensor_scalar(
                        out=xb[0:n, :], in0=xap, scalar1=rs[0:n, i:i + 1], scalar2=None, op0=ALU.mult),
                        (xkey, "rs"), (xbk,), cost=480.0)
                elif cast2 == "pool":
                    pool(lambda h, xb=xb, xap=xap, n=n, i=i: h.tensor_scalar(
                        out=xb[0:n, :], in0=xap, scalar1=rs[0:n, i:i + 1], scalar2=None, op0=ALU.mult),
                        (xkey, "rs"), (xbk,), cost=3600.0)
                else:
                    act(lambda h, xb=xb, xap=xap, n=n, i=i: h.activation(
                        out=xb[0:n, :], in_=xap, func=AF.Copy, scale=rs[0:n, i:i + 1]),
                        (xkey, "rs"), (xbk,), cost=1170.0)
            xkeys = tuple("xsb%d" % (i % 4) for i in range(nt))
            for c2 in range(4):
                b = kb.psum("T")
                pk = "ps%d" % b

                def tr(h, b=b, c2=c2, offs=tuple(offs)):
                    ins = None
                    for q in range(2):
                        c = 2 * c2 + q
                        for i, (r_, n) in enumerate(offs):
                            ins = h.transpose(out=PSB[b][:, q * 512 + r_:q * 512 + r_ + n],
                                              in_=xsb[i % 4][0:n, c * 128:(c + 1) * 128], identity=ident[0:n, 0:n])
                    return ins
                pe(tr, xkeys + ("ident",), (pk,), cost=2 * nt * 75.0)
                for q in range(2):
                    c = 2 * c2 + q
                    evac(hT[:, c, col0:col0 + W], PSB[b][:, q * 512:q * 512 + W],
                         (pk, "pv"), (hTkey,), scale=P(gname, c),
                         eng=(evac_eng[c % len(evac_eng)] if isinstance(evac_eng, tuple) else evac_eng))

        def mm_group(out_ap, pairs, reads, pk, n=None):
            def fn(h):
                ins = None
                nn = len(pairs)
                for i, (l, r) in enumerate(pairs):
                    ins = h.matmul(out_ap, l, r, start=(i == 0), stop=(i == nn - 1))
                return ins
            pe(fn, reads, (pk,), cost=len(pairs) * ((n or kb.W) * 0.42 + 4.0))

        st_kv = ExitStack()
        kT = st_kv.enter_context(nc.sbuf_tensor("kT", [128, 8, 256], BF16))
        vv = st_kv.enter_context(nc.sbuf_tensor("vv", [128, 2, D], BF16))
        st_lc = ExitStack()
        mixreg = st_lc.enter_context(nc.sbuf_tensor("mixreg", [128, 4 * (HALO + TOWN)], BF16))
        fence = st_lc.enter_context(nc.sbuf_tensor("fence", [128, 2], F32))
        mixL = mixreg[:, :].rearrange("p (j n) -> p j n", j=4)
        with ExitStack() as st:
            WK = WS1[:, :].rearrange("p (k n) -> p k n", k=8)
            WV = WS2[:, 0:8192].rearrange("p (k n) -> p k n", k=8)
            load_w(WK, w_kv, 1024, 0, "WS1")
            load_w(WV, w_kv, 1024, 0, "WS2", srccol0=1024)
            memt = mixreg.bitcast(F32)[:, 0:2048].rearrange("p (i n) -> p i n", i=2)
            mT = mixreg[:, 4096:6144].rearrange("p (k n) -> p k n", k=8)
            for i in range(2):
                kb.dma("sp", (lambda h, i=i: h.dma_start(out=memt[:, i, :], in_=memd[i * 128:(i + 1) * 128, :])),
                       (), ("memt%d" % i,), nbytes=524288)
            norm_T([(memt[:, i, :], "memt%d" % i, 128) for i in range(2)], 256, "gm", mT, "mT")
            for dc in range(8):
                b = kb.psum()
                mm_group(PS[b][:, 0:256], [(WK[:, k, dc * 128:(dc + 1) * 128], mT[:, k, :]) for k in range(8)],
                         ("WS1", "mT"), "ps%d" % b)
                evac(kT[:, dc, :], PS[b][:, 0:256], ("ps%d" % b,), ("kT",))
            for mc in range(2):
                for nh in range(2):
                    b = kb.psum()
                    mm_group(PS[b][:, :], [(mT[:, k, mc * 128:(mc + 1) * 128],
                                            WV[:, k, nh * 512:(nh + 1) * 512]) for k in range(8)],
                             ("WS2", "mT"), "ps%d" % b)
                    evac(vv[:, mc, nh * 512:(nh + 1) * 512], PS[b][:, :], ("ps%d" % b,), ("vv",))
            pool(lambda h: h.memset(fence[:], 0.0), (),
                 ("fence", "mT", "memt0", "memt1", "mixL", "WS1") + tuple("W1t%d" % i for i in range(8)), cost=150.0)
        load_w(WIN, w_in, 1536, 512, "WS0b", srccol0=512)

        CINW = 30 + HALO + TOWN
        cinb = WS2[:, 0:4 * CINW].rearrange("p (j n) -> p j n", j=4)
        with ExitStack() as st:
            hTs = [sb("hT%d" % i, [128, 8, 512], BF16, st) for i in range(2)]
            zx = sb("zx", [128, 4, 516], BF16, st)
            lxb = [sb("lxb%d" % i, [128, 512], BF16, st) for i in range(2)]
            hst = sb("hst", [128, 4], F32, st)
            X16 = X.bitcast(BF16)
            xs16 = [X16[:, i // 2, (i % 2) * 1024:(i % 2 + 1) * 1024] for i in range(6)]
            junk2 = X16[:, 5, 0:1024]
            lcbrow = sb("lcbrow_s", [1, 512], BF16, st)
            onesr = sb("onesr", [1, 512], BF16, st)
            kb.dma("pool", lambda h: h.dma_start(out=lcbrow[:], in_=lcbd), (), ("lcbrow",), nbytes=2048)
            pool(lambda h: h.memset(onesr[:], 1.0), (), ("onesr",), cost=300.0)
            D4 = sb("D4", [128, 16, 128], BF16, st)
            for kk in range(16):
                dve(lambda h, kk=kk: h.tensor_scalar(out=D4[:, kk, :], in0=identf[:], scalar1=P("lcw", kk),
                                                     scalar2=None, op0=ALU.mult), ("identf", "pv"), ("D4",), cost=200.0)
            NXS = 6

            def tmpb(i):
                s = 6 + i // 2
                return X[:, s, (i % 2) * 512:(i % 2) * 512 + 512], "X%dh%d" % (s, i % 2)

            pool(lambda h: h.memset(zx[:], 0.0), (), ("zx0", "zx1", "zx2", "zx3"))
            pool(lambda h: h.memset(hst[:], 0.0), (), ("hst0", "hst1", "hst2", "hst3"))
            pool(lambda h: h.memset(cinb[:, :, 0:30], 0.0), (), ("WS2",))

            tiles = [(512 * j, 512, "pre", j) for j in range(11)]
            tiles.append((5632, 480, "pre", 11))
            tiles.append((6112, 32, "own", 12))
            tiles += [(6144 + 512 * i, 512, "own", 13 + i) for i in range(4)]
            xslot = [0]
            prevW = [None]
            OM3 = [X[:, 12:14, :].rearrange("p s (h n) -> p (s h) n", h=2),
                   X[:, 3:5, :].rearrange("p s (h n) -> p (s h) n", h=2)]
            W1F = WS1.bitcast(F32)
            W1KEYS = tuple("W1t%d" % i for i in range(8))

            def tset(par, j):
                if par == 0:
                    return tmpb(4 + j), tmpb(8 + j), tmpb(12 + j)
                return ((W1F[:, j * 512:(j + 1) * 512], "W1t%d" % j),
                        (W1F[:, (4 + j) * 512:(5 + j) * 512], "W1t%d" % (4 + j)),
                        (X[:, 3 + j // 2, (j % 2) * 512:(j % 2) * 512 + 512], "X%dh%d" % (3 + j // 2, j % 2)))

            for ti, (row0, W, kind, mi) in enumerate(tiles):
                own = kind == "own"
                oc0 = row0 - 6112
                hT = hTs[ti % 2]
                hTk = "hT%d" % (ti % 2)
                kb.W = W
                srcs = []
                for (r, n) in ttiles(W):
                    s = xslot[0] % NXS
                    xslot[0] += 1
                    kb.dma("pool", (lambda h, s=s, r=r, n=n, row0=row0: h.dma_start(
                        out=xs16[s][0:n, :], in_=xall[row0 + r:row0 + r + n, :])), (), ("xs16_%d" % s,),
                        nbytes=n * 4096)
                    srcs.append((xs16[s][0:n, :], "xs16_%d" % s, n))
                norm_T(srcs, W, "g1", hT, hTk, evac_eng="dve", cast2="dve", sq_dve=True)
                for j in range(4):
                    (ti_, tik), (a_, ak), (om_, omk) = tset(ti % 2, j)
                    (tr_, trk) = tmpb(16 if j % 2 == 0 else 20)
                    zk = "zx%d" % j
                    b = kb.psum("Z")
                    pk = "ps%d" % b
                    mm_group(PS[b][:, 0:W],
                             [(WIN[:, k, j * 128:(j + 1) * 128], hT[:, k, 0:W]) for k in range(8)],
                             ("WS0a", hTk), pk)
                    if prevW[0] is not None:
                        pw = prevW[0]
                        dve(lambda h, j=j, pw=pw: h.tensor_copy(out=zx[:, j, 0:3], in_=zx[:, j, pw:pw + 3]),
                            (zk,), (zk,), cost=TINY)
                    dve(lambda h, j=j, b=b, W=W: h.tensor_copy(out=zx[:, j, 3:3 + W], in_=PS[b][:, 0:W]),
                        (pk, zk), (zk,))
                    bc = kb.psum("C")
                    pkc = "ps%d" % bc
                    mm_group(PS[bc][:, 0:W], [(D4[:, k * 4 + j, :], zx[:, j, k:k + W]) for k in range(4)]
                             + [(lcbrow[0:1, j * 128:(j + 1) * 128], onesr[0:1, 0:W])],
                             ("D4", zk, "lcbrow", "onesr"), pkc)
                    lb = lxb[j % 2]
                    lbk = "lxb%d" % (j % 2)
                    dve(lambda h, lb=lb, bc=bc, W=W: h.tensor_copy(out=lb[:, 0:W], in_=PS[bc][:, 0:W]), (pkc,), (lbk,))
                    b1 = kb.psum("G0")
                    b2 = kb.psum("G1")
                    pe(lambda h, b1=b1, j=j, lb=lb, W=W: h.matmul(PS[b1][:, 0:W], BD[:, (j * 2) * 128:(j * 2 + 1) * 128],
                                                                  lb[:, 0:W], start=True, stop=True),
                       ("BD", lbk), ("ps%d" % b1,), cost=W * 0.51 + 15.0)
                    pe(lambda h, b2=b2, j=j, lb=lb, W=W: h.matmul(PS[b2][:, 0:W],
                                                                  BD[:, (j * 2 + 1) * 128:(j * 2 + 2) * 128],
                                                                  lb[:, 0:W], start=True, stop=True),
                       ("BD", lbk), ("ps%d" % b2,), cost=W * 0.51 + 15.0)
                    act(lambda h, b1=b1, W=W, j=j, t=tr_: h.activation(out=t[:, 0:W], in_=PS[b1][:, 0:W],
                                                                       func=AF.Tanh, scale=0.5, bias=SM(S_HBA, j)),
                        ("ps%d" % b1, "sm"), (trk,), tag="s0")
                    act(lambda h, b2=b2, W=W, j=j, t=ti_: h.activation(out=t[:, 0:W], in_=PS[b2][:, 0:W],
                                                                       func=AF.Tanh, scale=0.5, bias=SM(S_HBX, j)),
                        ("ps%d" % b2, "sm"), (tik,), tag="s0")
                    act(lambda h, W=W, j=j, t=tr_, a=a_: h.activation(out=a[:, 0:W], in_=t[:, 0:W], func=AF.Exp,
                                                                      scale=SM(S_HCLA, j), bias=SM(S_HCLA, j)),
                        (trk, "sm"), (ak,), tag="s0")
                    act(lambda h, W=W, j=j, t=tr_, a=om_: h.activation(out=a[:, 0:W], in_=t[:, 0:W], func=AF.Exp,
                                                                       scale=SM(S_CLA, j), bias=SM(S_CLA, j)),
                        (trk, "sm"), (omk,), tag="s0")
                    dve(lambda h, W=W, t=ti_, bc=bc: h.scalar_tensor_tensor(
                        out=t[:, 0:W], in0=t[:, 0:W], scalar=1.0, in1=PS[bc][:, 0:W], op0=ALU.add, op1=ALU.mult),
                        (tik, pkc), (tik,))
                omks = tuple(tset(ti % 2, j)[2][1] for j in range(4))
                om3 = OM3[ti % 2]
                act(lambda h, W=W, mi=mi, om3=om3: h.activation(out=om3[:, :, 0:W], in_=om3[:, :, 0:W], func=AF.Sqrt,
                                                       scale=SM(S_NHM2, mi), bias=SM(S_HM2, mi)),
                    omks + ("sm",), omks, cost=224.0 + 0.833 * 4 * W, tag="sqrt")
                for j in range(4):
                    (ti_, tik), (a_, ak), (om_, omk) = tset(ti % 2, j)
                    (hb_, hbk) = tmpb(17 if j % 2 == 0 else 0)
                    dve(lambda h, W=W, t=ti_, m=om_: h.tensor_tensor(out=t[:, 0:W], in0=t[:, 0:W],
                                                                     in1=m[:, 0:W], op=ALU.mult),
                        (tik, omk), (tik,))
                    dve(lambda h, W=W, a=a_, u=ti_, hb=hb_, j=j: h.tensor_tensor_scan(
                        out=hb[:, 0:W], data0=a[:, 0:W], data1=u[:, 0:W], initial=hst[:, j:j + 1],
                        op0=ALU.mult, op1=ALU.add), (ak, tik, "hst%d" % j), (hbk,), cost=60.0 + 2.08 * W)
                    dve(lambda h, W=W, hb=hb_, j=j: h.tensor_copy(out=hst[:, j:j + 1], in_=hb[:, W - 1:W]),
                        (hbk,), ("hst%d" % j,), cost=TINY)
                    if own:
                        (gs, gsk), (sq, sqk) = tmpb(18), tmpb(19)
                        b = kb.psum("Z")
                        pk = "ps%d" % b
                        mm_group(PS[b][:, 0:W],
                                 [(WIN[:, k, 512 + j * 128:512 + (j + 1) * 128], hT[:, k, 0:W]) for k in range(8)],
                                 ("WS0b", hTk), pk)
                        act(lambda h, b=b, W=W, gs=gs: h.activation(out=gs[:, 0:W], in_=PS[b][:, 0:W],
                                                                    func=AF.Gelu_apprx_tanh), (pk,), (gsk,), tag="gelu")
                        dve(lambda h, W=W, hb=hb_, j=j, oc0=oc0, gs=gs: h.tensor_tensor(
                            out=mixL[:, j, oc0:oc0 + W], in0=gs[:, 0:W], in1=hb[:, 0:W], op=ALU.mult),
                            (gsk, hbk), ("mixL",))
                if own:
                    for j in range(4):
                        (tg, tgk) = tmpb(20 + j % 2)
                        ba_ = kb.psum("C")
                        bb_ = kb.psum("Z")
                        mm_group(PS[ba_][:, 0:W],
                                 [(WIN[:, k, 1024 + j * 128:1024 + (j + 1) * 128], hT[:, k, 0:W]) for k in range(8)],
                                 ("WS0b", hTk), "ps%d" % ba_)
                        mm_group(PS[bb_][:, 0:W],
                                 [(WIN[:, k, 1536 + j * 128:1536 + (j + 1) * 128], hT[:, k, 0:W]) for k in range(8)],
                                 ("WS0b", hTk), "ps%d" % bb_)
                        act(lambda h, bb_=bb_, W=W, tg=tg: h.activation(out=tg[:, 0:W], in_=PS[bb_][:, 0:W],
                                                                        func=AF.Tanh, scale=0.5),
                            ("ps%d" % bb_,), (tgk,), tag="s0")
                        dve(lambda h, ba_=ba_, W=W, tg=tg, j=j, oc0=oc0: h.scalar_tensor_tensor(
                            out=cinb[:, j, 30 + oc0:30 + oc0 + W], in0=tg[:, 0:W], scalar=1.0, in1=PS[ba_][:, 0:W],
                            op0=ALU.add, op1=ALU.mult), (tgk, "ps%d" % ba_), ("WS2",))
                prevW[0] = W
            for k in range(8):
                kb.dma("pool", (lambda h, k=k: h.dma_start(out=WOUT[:, k, :], in_=w_out[k * 128:(k + 1) * 128, :])),
                       (), ("WS1",) + W1KEYS, nbytes=524288)
            kb.barrier()

        own_tiles = [(0, HALO)] + [(HALO + 512 * i, 512) for i in range(4)]

        def xslots(oc0, W):
            if W == HALO:
                return [(0, HALO, 0)]
            base = 1 + (oc0 - HALO) // 128
            return [(base + i, 128, 128 * i) for i in range(4)]

        with ExitStack() as st:
            DIAG = WS0[:, :].rearrange("p (k n) -> p k n", k=128)
            for j in range(4):
                for k in range(31):
                    fnd = (lambda h, k=k, j=j: h.tensor_scalar(out=DIAG[:, k * 4 + j, :], in0=identf[:],
                                                               scalar1=P("ccw", k * 4 + j), scalar2=0.5,
                                                               op0=ALU.mult, op1=ALU.mult))
                    if k % 3 == 2:
                        pool(fnd, ("identf", "pv"), ("dg%d_%d" % (j, k),), cost=400.0)
                    else:
                        dve(fnd, ("identf", "pv"), ("dg%d_%d" % (j, k),), cost=200.0)
            cvs = sb("cvs", [128, 4, 512], F32, st)
            cvb = sb("cvb", [128, 4, 512], BF16, st)
            sqb = sb("sqb", [128, 4, 512], BF16, st)
            mean_s = sb("mean_s", [128, 512], F32, st)
            var_s = sb("var_s", [128, 512], F32, st)
            rstd_s = sb("rstd_s", [128, 512], F32, st)
            xc = [sb("xc%d" % i, [128, 512], F32, st) for i in range(2)]
            mixC = sb("mixC", [128, 4, 512], BF16, st)
            WQ = WS2[:, 0:8192].rearrange("p (k n) -> p k n", k=8)
            WO = WS1[:, :].rearrange("p (k n) -> p k n", k=8)
            WQST = X.bitcast(BF16)[:, 13:17, :].rearrange("p s (h n) -> p (s h) n", h=2)
            stkeys = tuple("X%d" % sl for sl in range(13, 17)) + tuple("X%dh%d" % (sl, hh_) for sl in range(13, 17)
                                                                       for hh_ in range(2))
            for k0 in (0, 4):
                kb.dma("pool", (lambda h, k0=k0: h.dma_start(
                    out=WQST[:, k0:k0 + 4, :],
                    in_=w_q[k0 * 128:(k0 + 4) * 128, :].rearrange("(k p) n -> p k n", p=128))),
                    (), stkeys, nbytes=2097152)
            for c2i, (oc0, W) in enumerate(own_tiles):
                kb.W = W
                for j in range(4):
                    b = kb.psum("A")
                    pk = "ps%d" % b
                    mm_group(PS[b][:, 0:W],
                             [(DIAG[:, k * 4 + j, :], cinb[:, j, oc0 + k:oc0 + k + W]) for k in range(31)],
                             tuple("dg%d_%d" % (j, k) for k in range(31)) + ("WS2",), pk)
                    act(lambda h, b=b, j=j, W=W: h.activation(out=cvs[:, j, 0:W], in_=PS[b][:, 0:W],
                                                              func=AF.Identity, bias=P("ccb", j)),
                        (pk, "pv"), ("cvs%d" % j,))
                    act(lambda h, b=b, j=j, W=W: h.activation(out=sqb[:, j, 0:W], in_=PS[b][:, 0:W],
                                                              func=AF.Square, bias=P("ccb", j)),
                        (pk, "pv"), ("sqb",))
                    dve(lambda h, j=j, W=W: h.tensor_copy(out=cvb[:, j, 0:W], in_=cvs[:, j, 0:W]),
                        ("cvs%d" % j,), ("cvb",))
                if c2i == len(own_tiles) - 1:
                    dve(lambda h: h.tensor_copy(out=WQ, in_=WQST), stkeys, ("WS2",), cost=4500.0)
                bm = kb.psum("M")
                bq = kb.psum("M")
                mm_group(PS[bm][:, 0:W], [(ones_s[:], cvb[:, j, 0:W]) for j in range(4)], ("ones_s", "cvb"),
                         "ps%d" % bm)
                mm_group(PS[bq][:, 0:W], [(ones_s[:], sqb[:, j, 0:W]) for j in range(4)], ("ones_s", "sqb"),
                         "ps%d" % bq)
                dve(lambda h, bm=bm, W=W: h.tensor_copy(out=mean_s[:, 0:W], in_=PS[bm][:, 0:W]),
                    ("ps%d" % bm,), ("mean_s",))
                dve(lambda h, W=W: h.tensor_tensor(out=var_s[:, 0:W], in0=mean_s[:, 0:W], in1=mean_s[:, 0:W],
                                                   op=ALU.mult), ("mean_s",), ("var_s",))
                dve(lambda h, bq=bq, W=W: h.scalar_tensor_tensor(out=var_s[:, 0:W], in0=PS[bq][:, 0:W], scalar=EPS,
                                                                 in1=var_s[:, 0:W], op0=ALU.add, op1=ALU.subtract),
                    ("ps%d" % bq, "var_s"), ("var_s",))
                act(lambda h, W=W: h.activation(out=var_s[:, 0:W], in_=var_s[:, 0:W], func=AF.Sqrt),
                    ("var_s",), ("var_s",), tag="sqrt")
                dve(lambda h, W=W: h.reciprocal(out=rstd_s[:, 0:W], in_=var_s[:, 0:W]), ("var_s",), ("rstd_s",),
                    cost=60.0 + 8.3 * W)
                for j in range(4):
                    x_ = xc[j % 2]
                    xk = "xc%d" % (j % 2)
                    dve(lambda h, j=j, W=W, x_=x_: h.tensor_tensor(out=x_[:, 0:W], in0=cvs[:, j, 0:W],
                                                                   in1=mean_s[:, 0:W], op=ALU.subtract),
                        ("cvs%d" % j, "mean_s"), (xk,))
                    dve(lambda h, W=W, x_=x_: h.tensor_tensor(out=x_[:, 0:W], in0=x_[:, 0:W], in1=rstd_s[:, 0:W],
                                                              op=ALU.mult), (xk, "rstd_s"), (xk,))
                    act(lambda h, j=j, W=W, x_=x_: h.activation(out=mixC[:, j, 0:W], in_=x_[:, 0:W], func=AF.Silu,
                                                                scale=P("lng", j), bias=P("lnb", j)),
                        (xk, "pv"), ("mixC",), tag="silu")
                for (slot, n, co) in xslots(oc0, W):
                    r0 = 6112 + oc0 + co
                    kb.dma("sp", (lambda h, slot=slot, n=n, r0=r0: h.dma_start(
                        out=X[0:n, slot, :], in_=xall[r0:r0 + n, :])), (),
                        ("X%d" % slot, "X%dh0" % slot, "X%dh1" % slot), nbytes=n * 4096)
                    for nh in range(2):
                        b = kb.psum("O")
                        pk = "ps%d" % b
                        pairs = [(mixL[:, k, oc0 + co:oc0 + co + n], WOUT[:, k, nh * 512:(nh + 1) * 512])
                                 for k in range(4)]
                        pairs += [(mixC[:, k, co:co + n], WOUT[:, 4 + k, nh * 512:(nh + 1) * 512]) for k in range(4)]
                        mm_group(PS[b][0:n, :], pairs, ("mixL", "mixC", "WS1"), pk, n=512)
                        dve(lambda h, b=b, n=n, slot=slot, nh=nh: h.tensor_tensor(
                            out=X[0:n, slot, nh * 512:(nh + 1) * 512], in0=PS[b][0:n, :],
                            in1=X[0:n, slot, nh * 512:(nh + 1) * 512], op=ALU.add),
                            (pk, "X%d" % slot), ("X%d" % slot,))
            load_w(WO, w_o, 1024, 0, "WS1")
            kb.barrier()
        st_lc.close()

        def write_out_debug():
            toks = []
            for i in range(16):
                toks.append(kb.dma("sp", (lambda h, i=i: h.dma_start(out=yd[i * 128:(i + 1) * 128, :],
                                                                    in_=X[:, 1 + i, :])), ("X%d" % (1 + i),), (), nbytes=524288))

        def ffn_views(gi):
            if gi % 2 == 0:
                return (WS0[:, 0:8192].rearrange("p (k n) -> p k n", k=8),
                        WS0[:, 8192:12288].rearrange("p (k n) -> p k n", k=4), "WS0a", "WS0a")
            return (WS1[:, :].rearrange("p (k n) -> p k n", k=8),
                    WS2[:, 0:4096].rearrange("p (k n) -> p k n", k=4), "WS1", "WS2")

        def ffn_load(gi):
            WU, WD, wk, wkd = ffn_views(gi)
            load_w(WU, w_up, 512, 0, wk, srccol0=gi * 512)
            load_w(WU, w_up, 512, 512, wk, srccol0=3072 + gi * 512)
            load_w(WD, w_down, 1024, 0, wkd, nk=4, row0=gi * 512)

        if DEBUG_STAGE == "C2":
            write_out_debug()
            st_kv.close()
        else:
            with ExitStack() as st:
                WQ = WS2[:, 0:8192].rearrange("p (k n) -> p k n", k=8)
                WO = WS1[:, :].rearrange("p (k n) -> p k n", k=8)
                ffn_load(0)
                h2Ts = [sb("h2T%d" % i, [128, 8, 512], BF16, st) for i in range(2)]
                qT = sb("qT", [128, 8, 512], BF16, st)
                pT = [sb("pT%d" % i, [128, 2, 512], BF16, st) for i in range(2)]
                rrec = [sb("rrec%d" % i, [128, 512], F32, st) for i in range(2)]
                oTs = [sb("oT%d" % i, [128, 8, 512], BF16, st) for i in range(2)]
                for ai, (oc0, W) in enumerate(own_tiles):
                    sl = xslots(oc0, W)
                    h2T = h2Ts[ai % 2]
                    h2k = "h2T%d" % (ai % 2)
                    oT = oTs[ai % 2]
                    oTk = "oT%d" % (ai % 2)
                    kb.W = W
                    norm_T([(X[0:n, s, :], "X%d" % s, n) for (s, n, co) in sl], W, "g2", h2T, h2k)
                    for dc in range(8):
                        b = kb.psum("Z")
                        mm_group(PS[b][:, 0:W], [(WQ[:, k, dc * 128:(dc + 1) * 128], h2T[:, k, 0:W]) for k in range(8)],
                                 ("WS2", h2k), "ps%d" % b)
                        evac(qT[:, dc, 0:W], PS[b][:, 0:W], ("ps%d" % b,), ("qT",))
                    for hh in range(4):
                        p_ = pT[hh % 2]
                        pkk = "pT%d" % (hh % 2)
                        rr = rrec[hh % 2]
                        rk = "rrec%d" % (hh % 2)
                        for mc in range(2):
                            b = kb.psum("C")
                            mm_group(PS[b][:, 0:W],
                                     [(kT[:, 2 * hh + dcc, mc * 128:(mc + 1) * 128], qT[:, 2 * hh + dcc, 0:W])
                                      for dcc in range(2)], ("kT", "qT"), "ps%d" % b)
                            act(lambda h, b=b, W=W, p_=p_, mc=mc: h.activation(out=p_[:, mc, 0:W], in_=PS[b][:, 0:W],
                                                                               func=AF.Exp, scale=1.0 / 16.0),
                                ("ps%d" % b,), (pkk,), tag="lnexp")
                        b = kb.psum("G0")
                        mm_group(PS[b][:, 0:W], [(ones1[:], p_[:, mc, 0:W]) for mc in range(2)], ("ones1", pkk),
                                 "ps%d" % b)
                        act(lambda h, b=b, W=W, rr=rr: h.activation(out=rr[:, 0:W], in_=PS[b][:, 0:W], func=AF.Ln),
                            ("ps%d" % b,), (rk,), tag="lnexp")
                        act(lambda h, W=W, rr=rr: h.activation(out=rr[:, 0:W], in_=rr[:, 0:W], func=AF.Exp, scale=-1.0),
                            (rk,), (rk,), tag="lnexp")
                        for dcc in range(2):
                            b = kb.psum("G1")
                            mm_group(PS[b][:, 0:W],
                                     [(vv[:, mc, hh * 256 + dcc * 128:hh * 256 + (dcc + 1) * 128], p_[:, mc, 0:W])
                                      for mc in range(2)], ("vv", pkk), "ps%d" % b)
                            dve(lambda h, b=b, W=W, rr=rr, hh=hh, dcc=dcc, oT=oT: h.tensor_tensor(
                                out=oT[:, 2 * hh + dcc, 0:W], in0=PS[b][:, 0:W], in1=rr[:, 0:W], op=ALU.mult),
                                ("ps%d" % b, rk), (oTk,))
                    for (slot, n, co) in sl:
                        for nh in range(2):
                            b = kb.psum("O")
                            mm_group(PS[b][0:n, :], [(oT[:, k, co:co + n], WO[:, k, nh * 512:(nh + 1) * 512])
                                                     for k in range(8)], (oTk, "WS1"), "ps%d" % b, n=512)
                            dve(lambda h, b=b, n=n, slot=slot, nh=nh: h.tensor_tensor(
                                out=X[0:n, slot, nh * 512:(nh + 1) * 512], in0=PS[b][0:n, :],
                                in1=X[0:n, slot, nh * 512:(nh + 1) * 512], op=ALU.add),
                                ("ps%d" % b, "X%d" % slot), ("X%d" % slot,))
                kb.barrier()
            st_kv.close()

            if DEBUG_STAGE == "A":
                write_out_debug()
            else:
                with ExitStack() as st:
                    h3T = sb("h3T", [128, 8, HALO + TOWN], BF16, st)
                    graw = [sb("graw%d" % i, [128, 516], F32, st) for i in range(2)]
                    acc = [sb("acc%d" % i, [128, 512], F32, st) for i in range(2)]
                    hid = [sb("hid%d" % i, [128, 4, 512], BF16, st) for i in range(2)]
                    carry = sb("carry", [128, 8], F32, st)
                    gf = sb("gf", [128, D], F32, st)
                    kb.dma("sp", lambda h: h.dma_start(out=gf[:], in_=gfd), (), ("gf",), nbytes=524288)
                    for (oc0, W) in own_tiles:
                        sl = xslots(oc0, W)
                        norm_T([(X[0:n, s, :], "X%d" % s, n) for (s, n, co) in sl], W, "g3", h3T, "h3T", col0=oc0)
                    for gi in range(6):
                        WU, WD, wk, wkd = ffn_views(gi)
                        if gi > 0:
                            ffn_load(gi)
                        for wi, (oc0, W) in enumerate(own_tiles):
                            halo = W == HALO
                            kb.W = W
                            hd = hid[wi % 2]
                            hk = "hid%d" % (wi % 2)
                            bus = {}

                            def stA(c, gi=gi, W=W, oc0=oc0, halo=halo, WU=WU, wk=wk):
                                hc = gi * 4 + c
                                g_ = graw[c % 2]
                                gk = "graw%d" % (c % 2)
                                a_ = acc[c % 2]
                                ak = "acc%d" % (c % 2)
                                bg = kb.psum("FG")
                                mm_group(PS[bg][:, 0:W],
                                         [(WU[:, k, c * 128:(c + 1) * 128], h3T[:, k, oc0:oc0 + W]) for k in range(8)],
                                         (wk, "h3T"), "ps%d" % bg)
                                act(lambda h, bg=bg, W=W, g_=g_: h.activation(out=g_[:, 2:2 + W], in_=PS[bg][:, 0:W],
                                                                              func=AF.Copy),
                                    ("ps%d" % bg,), (gk,))
                                if halo:
                                    dve(lambda h, g_=g_, c=c: h.tensor_scalar(
                                        out=carry[:, 2 * c:2 * c + 2], in0=g_[:, HALO:HALO + 2],
                                        scalar1=P("mask", 12), scalar2=None, op0=ALU.mult), (gk, "pv"), ("carry",),
                                        cost=TINY)
                                    return
                                dve(lambda h, g_=g_, c=c: h.tensor_copy(out=g_[:, 0:2], in_=carry[:, 2 * c:2 * c + 2]),
                                    ("carry",), (gk,), cost=TINY)
                                dve(lambda h, g_=g_, c=c, W=W: h.tensor_copy(out=carry[:, 2 * c:2 * c + 2],
                                                                             in_=g_[:, W:W + 2]), (gk,), ("carry",), cost=TINY)
                                act(lambda h, g_=g_, a_=a_, W=W, hc=hc: h.activation(
                                    out=a_[:, 0:W], in_=g_[:, 2:2 + W], func=AF.Identity,
                                    scale=P("fcw", 2 * 24 + hc), bias=P("fcb", hc)), (gk, "pv"), (ak,))
                                for k in range(2):
                                    dve(lambda h, g_=g_, a_=a_, W=W, hc=hc, k=k: h.scalar_tensor_tensor(
                                        out=a_[:, 0:W], in0=g_[:, k:k + W], scalar=P("fcw", k * 24 + hc),
                                        in1=a_[:, 0:W], op0=ALU.mult, op1=ALU.add), (gk, ak, "pv"), (ak,))
                                act(lambda h, a_=a_, W=W: h.activation(out=a_[:, 0:W], in_=a_[:, 0:W],
                                                                       func=AF.Gelu_apprx_tanh), (ak,), (ak,), tag="gelu")

                            def stB(c, W=W, oc0=oc0, WU=WU, wk=wk, hd=hd, hk=hk):
                                a_ = acc[c % 2]
                                ak = "acc%d" % (c % 2)
                                bu = kb.psum("FU")
                                mm_group(PS[bu][:, 0:W],
                                         [(WU[:, k, 512 + c * 128:512 + (c + 1) * 128], h3T[:, k, oc0:oc0 + W])
                                          for k in range(8)], (wk, "h3T"), "ps%d" % bu)
                                dve(lambda h, a_=a_, W=W, bu=bu, hd=hd, c=c: h.tensor_tensor(
                                    out=hd[:, c, 0:W], in0=a_[:, 0:W], in1=PS[bu][:, 0:W], op=ALU.mult),
                                    (ak, "ps%d" % bu), (hk,))

                            if halo:
                                for c in range(4):
                                    stA(c)
                            else:
                                stA(0)
                                stA(1)
                                stB(0)
                                stA(2)
                                stB(1)
                                stA(3)
                                stB(2)
                                stB(3)
                            if halo:
                                continue
                            for (slot, n, co) in xslots(oc0, W):
                                for nh in range(2):
                                    b = kb.psum("FD")
                                    mm_group(PS[b][0:n, :],
                                             [(hd[:, c, co:co + n], WD[:, c, nh * 512:(nh + 1) * 512])
                                              for c in range(4)], (hk, wkd), "ps%d" % b, n=512)
                                    dve(lambda h, b=b, n=n, slot=slot, nh=nh: h.tensor_tensor(
                                        out=X[0:n, slot, nh * 512:(nh + 1) * 512], in0=PS[b][0:n, :],
                                        in1=X[0:n, slot, nh * 512:(nh + 1) * 512], op=ALU.add),
                                        ("ps%d" % b, "X%d" % slot), ("X%d" % slot,))
                    toks = []
                    for i in range(16):
                        slot = 1 + i
                        f_ = X[:, slot, :]
                        fk = "X%d" % slot
                        c4 = i % 4
                        act(lambda h, slot=slot, c4=c4: h.activation(out=junk[:, :], in_=X[:, slot, :], func=AF.Square,
                                                                     accum_out=ss[:, c4:c4 + 1]),
                            ("X%d" % slot,), ("ss%d" % c4, "junk"), cost=1170.0)
                        pool(lambda h, c4=c4: h.tensor_scalar(out=rs[:, c4:c4 + 1], in0=ss[:, c4:c4 + 1],
                                                              scalar1=1.0 / D, scalar2=EPS, op0=ALU.mult, op1=ALU.add),
                             ("ss%d" % c4,), ("rs",), cost=200.0)
                        pool(lambda h, c4=c4: h.tensor_tensor(out=rs[:, c4:c4 + 1], in0=rs[:, c4:c4 + 1],
                                                              in1=neghT[:, 0:1], op=ALU.pow), ("rs", "neghT"), ("rs",),
                             cost=400.0)
                        dve(lambda h, slot=slot, c4=c4, f_=f_: h.scalar_tensor_tensor(
                            out=f_, in0=f_, scalar=rs[:, c4:c4 + 1], in1=gf[:, :],
                            op0=ALU.mult, op1=ALU.mult), (fk, "rs", "gf"), (fk,), cost=1125.0)
                        toks.append(kb.dma("sp", (lambda h, i=i, f_=f_: h.dma_start(
                            out=yd[i * 128:(i + 1) * 128, :], in_=f_)), (fk,), (), nbytes=524288))

        kb.finalize()
        with nc.Block() as block:
            @block.tensor
            def _(h):
                kb.replay(h, "pe")

            @block.scalar
            def _(h):
                kb.replay(h, "act")

            @block.vector
            def _(h):
                kb.replay(h, "dve")

            @block.gpsimd
            def _(h):
                kb.replay(h, "pool")

            @block.sync
            def _(h):
                kb.replay(h, "sp")
    return nc


def _pack_params(inp, core):
    k = core % 4
    pvv = np.zeros((128, NPV), np.float32)

    def put(name, arr2d):
        o = PCOLS[name]
        pvv[:, o:o + arr2d.shape[0]] = arr2d.T

    ch = lambda v, n: np.asarray(v, np.float32).reshape(n, 128)
    put("g1", ch(inp["mix_norm_g"][0], 8))
    put("lcw", np.asarray(inp["lru_conv_w"][0], np.float32).reshape(4 * 4, 128))
    put("lcb", ch(inp["lru_conv_b"][0], 4))
    put("ba", ch(inp["lru_b_a"][0], 4))
    put("bx", ch(inp["lru_b_x"][0], 4))
    put("lam", ch(inp["lru_lambda"][0], 4))
    put("ccw", np.asarray(inp["conf_conv_w"][0], np.float32).reshape(31 * 4, 128))
    put("ccb", ch(inp["conf_conv_b"][0], 4))
    put("lng", ch(inp["conf_ln_g"][0], 4))
    put("lnb", ch(inp["conf_ln_b"][0], 4))
    put("g2", ch(inp["xa_norm_g"][0], 8))
    put("gm", ch(inp["mem_norm_g"][0], 8))
    put("g3", ch(inp["ffn_norm_g"][0], 8))
    put("fcw", np.asarray(inp["ffn_conv_w"][0], np.float32).reshape(3 * 24, 128))
    put("fcb", ch(inp["ffn_conv_b"][0], 24))
    lo = 2048 * k - 6144
    m = np.ones(17, np.float32)
    for j in range(12):
        m[j] = 1.0 if lo + 512 * j >= 0 else 0.0
    m[12] = m[11]
    pvv[:, PCOLS["mask"]:PCOLS["mask"] + 17] = m[None, :]
    return pvv


def kernel(**inputs):
    inp = {k: np.asarray(v) for k, v in inputs.items()}
    x = inp["x"].astype(np.float32, copy=False)
    mem = inp["mem"].astype(np.float32, copy=False)
    nc = build_nc()
    gfin = np.ascontiguousarray(np.broadcast_to(inp["final_norm_g"].astype(np.float32)[None, :], (128, D)))
    bd = np.zeros((128, 4, 2, 128), np.float32)
    for j in range(4):
        for s, nm in enumerate(("lru_w_a", "lru_w_x")):
            w = inp[nm][0]
            bd[0:64, j, s, 0:64] = w[2 * j]
            bd[64:128, j, s, 64:128] = w[2 * j + 1]
    bd = bd.reshape(128, 1024)
    shared = dict(gfin=gfin, bd=bd, lcbrow=np.ascontiguousarray(inp["lru_conv_b"][0].astype(np.float32).reshape(1, 512)), w_in=np.ascontiguousarray(inp["w_in"][0]), w_out=np.ascontiguousarray(inp["w_out"][0]),
                  w_q=np.ascontiguousarray(inp["w_q"][0]), w_kv=np.ascontiguousarray(inp["w_kv"][0]),
                  w_o=np.ascontiguousarray(inp["w_o"][0]), w_up=np.ascontiguousarray(inp["w_up"][0]),
                  w_down=np.ascontiguousarray(inp["w_down"][0]))
    in_maps = []
    for c in range(8):
        b, k = c // 4, c % 4
        t0 = 2048 * k
        lo = t0 - 6144
        xa = np.zeros((NROWS, D), np.float32)
        s = max(0, lo)
        xa[s - lo:] = x[b, s:t0 + 2048]
        d = dict(shared)
        d["xall"] = xa
        d["mem"] = np.ascontiguousarray(mem[b])
        d["pv"] = _pack_params(inp, c)
        in_maps.append(d)
    res = run_bass_kernel_spmd(nc, in_maps, core_ids=list(range(8)))
    out = np.zeros((2, SEQ, D), np.float32)
    for c in range(8):
        b, k = c // 4, c % 4
        out[b, 2048 * k:2048 * (k + 1)] = res.results[c]["y"]
    return out
```

```python
import numpy as np
from contextlib import ExitStack
import concourse.bass as bass
import concourse.mybir as mybir
from concourse.bass_utils import run_bass_kernel_spmd

F32 = mybir.dt.float32
BF16 = mybir.dt.bfloat16
I32 = mybir.dt.int32
AF = mybir.ActivationFunctionType
ALU = mybir.AluOpType

D = 1024
SEQ = 8192
TOWN = 2048
HALO = 32
NROWS = 8192
EPS = 1e-6
GK = 0.7978845608028654

PCOLS = {}
_o = 0
for _n, _w in (("g1", 8), ("lcw", 16), ("lcb", 4), ("ba", 4), ("bx", 4), ("lam", 4), ("ccw", 124),
               ("ccb", 4), ("lng", 4), ("lnb", 4), ("g2", 8), ("gm", 8), ("g3", 8), ("fcw", 72),
               ("fcb", 24), ("mask", 17)):
    PCOLS[_n] = _o
    _o += _w
NPV = _o

DEBUG_STAGE = None


class KB:
    NS = {"sp": 8, "pool": 16}
    SLACK = 0.0
    ENG = ("pe", "act", "dve", "pool", "sp")

    def __init__(self, nc, es):
        self.nc = nc
        self.sem = {n: es.enter_context(nc.semaphore("s_" + n)) for n in self.ENG}
        self.dsem = {q: [es.enter_context(nc.semaphore("d_%s%d" % (q, i))) for i in range(self.NS[q])]
                     for q in ("sp", "pool")}
        self.ops = []
        self.lastw = {}
        self.readers = {}
        self.phase = 0
        self.psn = 0
        self.streams = None
        self._dops = {"sp": [], "pool": []}
        self.prole = {}
        self.rr_only = False
        self.W = 512

    def _rec(self, eng, fn, reads, writes, cost, kind, tag=None, nbytes=0):
        preds = set()
        for k in reads:
            t = self.lastw.get(k)
            if t is not None:
                preds.add(t)
        for k in writes:
            t = self.lastw.get(k)
            if t is not None:
                preds.add(t)
            preds.update(self.readers.get(k, ()))
        i = len(self.ops)
        self.ops.append(dict(eng=eng, fn=fn, preds=preds, cost=float(cost), kind=kind, tag=tag, nbytes=nbytes,
                             phase=self.phase))
        for k in writes:
            self.lastw[k] = i
            self.readers[k] = set()
        for k in reads:
            self.readers.setdefault(k, set()).add(i)
        return i

    def op(self, eng, fn, reads=(), writes=(), cost=600.0, tag=None):
        return self._rec(eng, fn, reads, writes, cost, "op", tag)

    def dma(self, q, fn, reads=(), writes=(), nbytes=65536):
        return self._rec(q, fn, reads, writes, 0.0, "dma", None, nbytes)

    def barrier(self):
        self.phase += 1

    POOLS = {"FG": (0, 1, 2), "FU": (3, 4), "FD": (5, 6, 7), "T": (0, 1), "Z": (2, 3), "C": (4, 5), "G0": (6,), "G1": (7,),
             "A": (0, 1, 2, 3), "M": (4, 5), "O": (6, 7)}

    def psum(self, role=None):
        if role is None or self.rr_only:
            b = self.psn
            self.psn = (self.psn + 1) % 8
            return b
        pool = self.POOLS[role]
        k = self.prole.get(role, 0)
        self.prole[role] = k + 1
        return pool[k % len(pool)]

    def finalize(self, window=56):
        ops = self.ops
        nph = self.phase + 1
        finish = [None] * len(ops)
        pos = [None] * len(ops)
        streams = {n: [] for n in self.ENG}
        eng_free = {n: 0.0 for n in self.ENG}
        cnt = {n: 0 for n in self.ENG}
        dn = {"sp": 0, "pool": 0}
        dhist = {"sp": [], "pool": []}
        seen = {n: {} for n in self.ENG}
        act_tag = [None]
        pipe_free = [0.0]
        BW = 170.0
        tnow = 0.0
        succ = [[] for _ in ops]
        for i, o in enumerate(ops):
            for p in o["preds"]:
                if ops[p]["phase"] == o["phase"]:
                    succ[p].append(i)
        blevel = [0.0] * len(ops)
        for i in range(len(ops) - 1, -1, -1):
            o = ops[i]
            c = o["cost"] if o["kind"] != "dma" else (2500.0 + o["nbytes"] / BW)
            m = 0.0
            for q in succ[i]:
                if blevel[q] > m:
                    m = blevel[q]
            blevel[i] = c + 200.0 + m
        for ph in range(nph):
            pend = {n: [] for n in self.ENG}
            for i, o in enumerate(ops):
                if o["phase"] == ph:
                    pend[o["eng"]].append(i)
            nleft = sum(len(v) for v in pend.values())
            t_phase = max(eng_free.values()) if ph > 0 else 0.0
            if ph > 0:
                fin_prev = [finish[i] for i, o in enumerate(ops) if o["phase"] == ph - 1 and o["kind"] != "dma"]
                if fin_prev:
                    t_phase = max(t_phase, max(fin_prev))
                for n in self.ENG:
                    waits = []
                    for m in self.ENG:
                        if m != n and cnt[m] > seen[n].get("s_" + m, 0):
                            waits.append((self.sem[m], cnt[m]))
                            seen[n]["s_" + m] = cnt[m]
                    if waits:
                        streams[n].append((waits, None, None, 0))
                    eng_free[n] = max(eng_free[n], t_phase)
            while nleft > 0:
                best = None
                for n in self.ENG:
                    pl = pend[n]
                    if not pl:
                        continue
                    cb = None
                    for i in pl[:window]:
                        o = ops[i]
                        ready = 0.0
                        ok = True
                        for p in o["preds"]:
                            f = finish[p]
                            if f is None:
                                ok = False
                                break
                            lat = 260.0 if ops[p]["eng"] != n else 110.0
                            if ops[p]["eng"] == n and n == "pe":
                                lat = 0.0
                                f = f - ops[p]["cost"]
                            if f + lat > ready:
                                ready = f + lat
                        if not ok:
                            continue
                        st = max(eng_free[n], ready)
                        if o["kind"] == "dma":
                            hist = dhist[n]
                            if len(hist) >= self.NS[n]:
                                pi = self._dops[n][len(hist) - self.NS[n]]
                                st = max(st, finish[pi])
                        if n == "act" and o["tag"] is not None and o["tag"] != act_tag[0]:
                            st += 2700.0
                        if st <= eng_free[n] + self.SLACK:
                            key = (0, -blevel[i], st)
                        else:
                            key = (1, st, -blevel[i])
                        if cb is None or key < cb[2]:
                            cb = (st, i, key)
                    if cb is not None and (best is None or cb[0] < best[0] - 1e-9):
                        best = (cb[0], cb[1], n)
                assert best is not None, "scheduler deadlock"
                st, i, n = best
                o = ops[i]
                pend[n].remove(i)
                nleft -= 1
                need = {}
                for p in o["preds"]:
                    if ops[p]["eng"] == n and n == "pe":
                        continue
                    sem, val, name = pos[p]
                    if seen[n].get(name, 0) >= val:
                        continue
                    if name not in need or need[name][1] < val:
                        need[name] = (sem, val)
                if o["kind"] == "dma":
                    k = dn[n]
                    slot = k % self.NS[n]
                    val = (k // self.NS[n] + 1) * 16
                    name = "d_%s%d" % (n, slot)
                    if val > 16 and seen[n].get(name, 0) < val - 16:
                        need[name] = (self.dsem[n][slot], val - 16)
                    dn[n] += 1
                    dhist[n].append((slot, val))
                    self._dops[n].append(i)
                    pos[i] = (self.dsem[n][slot], val, name)
                    issue = 100.0 if n == "sp" else 620.0
                    tdat = o["nbytes"] / (BW if n == "sp" else 220.0)
                    pstart = max(st + issue, pipe_free[0])
                    pipe_free[0] = pstart + tdat
                    finish[i] = pstart + tdat + 1800.0
                    eng_free[n] = st + issue
                    inc = 16
                    semo = self.dsem[n][slot]
                else:
                    cnt[n] += 1
                    pos[i] = (self.sem[n], cnt[n], "s_" + n)
                    finish[i] = st + o["cost"]
                    eng_free[n] = finish[i]
                    if n == "act" and o["tag"] is not None:
                        act_tag[0] = o["tag"]
                    inc = 1
                    semo = self.sem[n]
                for name, (sem, val) in need.items():
                    seen[n][name] = val
                streams[n].append((list(need.values()), o["fn"], semo, inc))
        waits = []
        for q in ("sp", "pool"):
            for j in range(self.NS[q]):
                done = [v for (slot, v) in dhist[q] if slot == j]
                if done:
                    waits.append((self.dsem[q][j], done[-1]))
        streams["sp"].append((waits, None, None, 0))
        self.streams = streams
        self.sim_time = max(eng_free.values())

    _dops = None

    def replay(self, h, name):
        for waits, fn, sem, inc in self.streams[name]:
            for s, v in waits:
                h.wait_ge(s, v)
            if fn is not None:
                ins = fn(h)
                ins.then_inc(sem, inc)


def build_nc():
    nc = bass.Bass("TRN2", target_bir_lowering=False)
    dt = lambda n, s, kind="ExternalInput": nc.dram_tensor(n, s, F32, kind=kind).ap()
    xall = dt("xall", [NROWS, D])
    memd = dt("mem", [256, D])
    pvd = dt("pv", [128, NPV])
    gfd = dt("gfin", [128, D])
    bdd = dt("bd", [128, 4 * 2 * 128])
    lcbd = dt("lcbrow", [1, 512])
    w_in = dt("w_in", [D, 2048])
    w_out = dt("w_out", [D, D])
    w_q = dt("w_q", [D, D])
    w_kv = dt("w_kv", [D, 2048])
    w_o = dt("w_o", [D, D])
    w_up = dt("w_up", [D, 6144])
    w_down = dt("w_down", [3072, D])
    yd = dt("y", [TOWN, D], kind="ExternalOutput")

    with ExitStack() as es:
        kb = KB(nc, es)
        sb = lambda n, s, d=F32, st=None: (st or es).enter_context(nc.sbuf_tensor(n, s, d))
        PS = [es.enter_context(nc.psum_tensor("ps%d" % i, [128, 512], F32)) for i in range(8)]
        PSB = [p.bitcast(BF16) for p in PS]

        X = sb("X", [128, 17, D])
        WS0 = sb("WS0", [128, 16384], BF16)
        WS1 = sb("WS1", [128, 8192], BF16)
        WS2 = sb("WS2", [128, 8448], BF16)
        pv = sb("pvs", [128, NPV])
        ident = sb("ident", [128, 128], BF16)
        identf = sb("identf", [128, 128])
        iot = sb("iot", [128, 128], I32)
        ones_s = sb("ones_s", [128, 128], BF16)
        ones1 = sb("ones1", [128, 128], BF16)
        neghT = sb("neghT", [128, 16])
        sm = sb("sm", [128, 96])
        BD = sb("BD", [128, 1024], BF16)
        ss = sb("ss", [128, 8])
        rs = sb("rs", [128, 8])
        junk = sb("junk", [128, D], BF16)
        xsb = [sb("xsb%d" % i, [128, D], BF16) for i in range(4)]

        P = lambda name, i=0: pv[:, PCOLS[name] + i:PCOLS[name] + i + 1]
        S_CLA, S_HCLA, S_HBA, S_HBX, S_HM2, S_NHM2, S_TMP = 0, 4, 8, 12, 16, 33, 50

        def SM(c, i=0):
            return sm[:, c + i:c + i + 1]

        pe = lambda fn, r=(), w=(), cost=300.0: kb.op("pe", fn, r, w, cost)
        act = lambda fn, r=(), w=(), cost=None, tag=None: kb.op(
            "act", fn, r, w, (260.0 + 0.85 * kb.W) if cost is None else cost, tag)
        dve = lambda fn, r=(), w=(), cost=None: kb.op("dve", fn, r, w, (110.0 + 1.15 * kb.W) if cost is None else cost)
        pool = lambda fn, r=(), w=(), cost=None: kb.op("pool", fn, r, w,
                                                      (150.0 + 3.4 * kb.W) if cost is None else cost)
        TINY = 120.0

        kb.dma("sp", lambda h: h.dma_start(out=pv[:], in_=pvd), (), ("pv",), nbytes=128 * NPV * 4)
        kb.dma("pool", lambda h: h.dma_start(out=BD[:], in_=bdd), (), ("BD",), nbytes=128 * 1024 * 4)
        pool(lambda h: h.iota(iot[:], [[1, 128]], base=0, channel_multiplier=-1), (), ("iot",))
        pool(lambda h: h.memset(ones_s[:], 1.0 / 512.0), (), ("ones_s",))
        pool(lambda h: h.memset(ones1[:], 1.0), (), ("ones1",))
        pool(lambda h: h.memset(neghT[:], -0.5), (), ("neghT",))
        pool(lambda h: h.memset(ss[:], 1.0), (), tuple("ss%d" % i for i in range(8)))
        pool(lambda h: h.memset(rs[:], 1.0), (), ("rs",))
        dve(lambda h: h.tensor_scalar(out=ident[:], in0=iot[:], scalar1=0.0, scalar2=None, op0=ALU.is_equal),
            ("iot",), ("ident",))
        dve(lambda h: h.tensor_scalar(out=identf[:], in0=iot[:], scalar1=0.0, scalar2=None, op0=ALU.is_equal),
            ("iot",), ("identf",))

        WIN = WS0[:, :].rearrange("p (k n) -> p k n", k=8)
        WOUT = WS1[:, :].rearrange("p (k n) -> p k n", k=8)

        def load_w(dst3, src, ncols, col0, keyname, srccol0=0, nk=8, row0=0):
            for c in range(0, ncols, 1024):
                w = min(1024, ncols - c)
                for k0 in range(0, nk, 4):
                    k1 = min(nk, k0 + 4)
                    kb.dma("pool",
                           (lambda h, k0=k0, k1=k1, c=c, w=w: h.dma_start(
                               out=dst3[:, k0:k1, col0 + c:col0 + c + w],
                               in_=src[row0 + k0 * 128:row0 + k1 * 128, srccol0 + c:srccol0 + c + w].rearrange(
                                   "(k p) n -> p k n", p=128))),
                           (), (keyname,), nbytes=128 * w * 4 * (k1 - k0))

        load_w(WIN, w_in, 512, 0, "WS0a")

        T = lambda i: sm[:, S_TMP + 4 * i:S_TMP + 4 * i + 4]
        lam = pv[:, PCOLS["lam"]:PCOLS["lam"] + 4]
        k_sm = "sm"
        dve(lambda h: h.tensor_scalar(out=T(0), in0=lam, scalar1=-1.0, scalar2=0.0, op0=ALU.mult, op1=ALU.max),
            ("pv",), (k_sm,))
        dve(lambda h: h.tensor_scalar(out=T(1), in0=lam, scalar1=-1.0, scalar2=None, op0=ALU.mult), ("pv",), (k_sm,))
        dve(lambda h: h.tensor_tensor(out=T(1), in0=T(1), in1=lam, op=ALU.max), ("pv", k_sm), (k_sm,))
        act(lambda h: h.activation(out=T(1), in_=T(1), func=AF.Exp, scale=-1.0), (k_sm,), (k_sm,), tag="s0")
        dve(lambda h: h.tensor_scalar(out=T(2), in0=T(1), scalar1=2.0, scalar2=None, op0=ALU.add), (k_sm,), (k_sm,))
        dve(lambda h: h.reciprocal(out=T(2), in_=T(2)), (k_sm,), (k_sm,))
        dve(lambda h: h.tensor_tensor(out=T(1), in0=T(1), in1=T(2), op=ALU.mult), (k_sm,), (k_sm,))
        dve(lambda h: h.tensor_tensor(out=T(2), in0=T(1), in1=T(1), op=ALU.mult), (k_sm,), (k_sm,))
        dve(lambda h: h.tensor_scalar(out=T(3), in0=T(2), scalar1=1.0 / 11, scalar2=1.0 / 9, op0=ALU.mult,
                                      op1=ALU.add), (k_sm,), (k_sm,))
        for cc in (1.0 / 7, 1.0 / 5, 1.0 / 3, 1.0):
            dve(lambda h: h.tensor_tensor(out=T(3), in0=T(3), in1=T(2), op=ALU.mult), (k_sm,), (k_sm,))
            dve(lambda h, cc=cc: h.tensor_scalar(out=T(3), in0=T(3), scalar1=cc, scalar2=None, op0=ALU.add),
                (k_sm,), (k_sm,))
        dve(lambda h: h.tensor_tensor(out=T(3), in0=T(3), in1=T(1), op=ALU.mult), (k_sm,), (k_sm,))
        dve(lambda h: h.scalar_tensor_tensor(out=T(3), in0=T(3), scalar=2.0, in1=T(0), op0=ALU.mult, op1=ALU.add),
            (k_sm,), (k_sm,))
        dve(lambda h: h.tensor_scalar(out=sm[:, S_CLA:S_CLA + 4], in0=T(3), scalar1=-8.0, scalar2=None,
                                      op0=ALU.mult), (k_sm,), (k_sm,))
        dve(lambda h: h.tensor_scalar(out=sm[:, S_HCLA:S_HCLA + 4], in0=T(3), scalar1=-4.0, scalar2=None,
                                      op0=ALU.mult), (k_sm,), (k_sm,))
        dve(lambda h: h.tensor_scalar(out=sm[:, S_HBA:S_HBA + 4], in0=pv[:, PCOLS["ba"]:PCOLS["ba"] + 4],
                                      scalar1=0.5, scalar2=None, op0=ALU.mult), ("pv",), (k_sm,))
        dve(lambda h: h.tensor_scalar(out=sm[:, S_HBX:S_HBX + 4], in0=pv[:, PCOLS["bx"]:PCOLS["bx"] + 4],
                                      scalar1=0.5, scalar2=None, op0=ALU.mult), ("pv",), (k_sm,))
        mk = pv[:, PCOLS["mask"]:PCOLS["mask"] + 17]
        dve(lambda h: h.tensor_scalar(out=sm[:, S_HM2:S_HM2 + 17], in0=mk, scalar1=0.25, scalar2=None,
                                      op0=ALU.mult), ("pv",), (k_sm,))
        dve(lambda h: h.tensor_scalar(out=sm[:, S_NHM2:S_NHM2 + 17], in0=mk, scalar1=-0.25, scalar2=None,
                                      op0=ALU.mult), ("pv",), (k_sm,))

        evac_rr = [0]

        def evac(out_ap, in_ap, reads, writes, scale=None, eng=None):
            if eng is None:
                eng = ("act", "dve")[evac_rr[0] % 2]
                evac_rr[0] += 1
            if eng == "act":
                if scale is None:
                    act(lambda h: h.activation(out=out_ap, in_=in_ap, func=AF.Copy), reads, writes)
                else:
                    act(lambda h: h.activation(out=out_ap, in_=in_ap, func=AF.Copy, scale=scale), reads, writes)
            else:
                if scale is None:
                    dve(lambda h: h.tensor_copy(out=out_ap, in_=in_ap), reads, writes)
                else:
                    dve(lambda h: h.tensor_scalar(out=out_ap, in0=in_ap, scalar1=scale, scalar2=None,
                                                  op0=ALU.mult), reads, writes)

        def ttiles(W):
            out = []
            r = 0
            while r < W:
                n = min(128, W - r)
                out.append((r, n))
                r += n
            return out

        def norm_T(src_tiles, W, gname, hT, hTkey, col0=0, cast2="act", evac_eng=None, sq_dve=False):
            nt = len(src_tiles)
            kb.W = W
            for i, (xap, xkey, n) in enumerate(src_tiles):
                if sq_dve and i == 3:
                    dve(lambda h, xap=xap, n=n, i=i: h.scalar_tensor_tensor(
                        out=junk2[0:n, :], in0=xap, scalar=1.0, in1=xap, op0=ALU.mult, op1=ALU.mult,
                        accum_out=ss[0:n, i:i + 1]), (xkey,), ("ss%d" % i, "junk2"), cost=1250.0)
                    continue
                jb, jk = (junk3, "junk3") if (sq_dve and i == 1) else (junk, "junk")
                act(lambda h, xap=xap, n=n, i=i, jb=jb: h.activation(out=jb[0:n, :], in_=xap, func=AF.Square,
                                                                     accum_out=ss[0:n, i:i + 1]),
                    (xkey,), ("ss%d" % i, jk), cost=1170.0)
            sskeys = tuple("ss%d" % i for i in range(nt))
            pool(lambda h: h.tensor_scalar(out=rs[:, 0:nt], in0=ss[:, 0:nt], scalar1=1.0 / D, scalar2=EPS,
                                           op0=ALU.mult, op1=ALU.add), sskeys, ("rs",), cost=200.0)
            pool(lambda h: h.tensor_tensor(out=rs[:, 0:nt], in0=rs[:, 0:nt], in1=neghT[:, 0:nt], op=ALU.pow),
                 ("rs", "neghT"), ("rs",), cost=200.0 + 170.0 * nt)
            r = 0
            offs = []
            for i, (xap, xkey, n) in enumerate(src_tiles):
                xb = xsb[i % 4]
                xbk = "xsb%d" % (i % 4)
                offs.append((r, n))
                r += n
                if i % 2 == 0:
                    dve(lambda h, xb=xb, xap=xap, n=n, i=i: h.tensor_scalar(
                        out=xb[0:n, :], in0=xap, scalar1=rs[0:n, i:i + 1], scalar2=None, op0=ALU.mult),
                        (xkey, "rs"), (xbk,), cost=1125.0)
                elif cast2 == "dve":
                    dve(lambda h, xb=xb, xap=xap, n=n, i=i: h.tensor_scalar(
                        out=xb[0:n, :], in0=xap, scalar1=rs[0:n, i:i + 1], scalar2=None, op0=ALU.mult),
                        (xkey, "rs"), (xbk,), cost=700.0)
                elif cast2 == "pool":
                    pool(lambda h, xb=xb, xap=xap, n=n, i=i: h.tensor_scalar(
                        out=xb[0:n, :], in0=xap, scalar1=rs[0:n, i:i + 1], scalar2=None, op0=ALU.mult),
                        (xkey, "rs"), (xbk,), cost=3600.0)
                else:
                    act(lambda h, xb=xb, xap=xap, n=n, i=i: h.activation(
                        out=xb[0:n, :], in_=xap, func=AF.Copy, scale=rs[0:n, i:i + 1]),
                        (xkey, "rs"), (xbk,), cost=1170.0)
            xkeys = tuple("xsb%d" % (i % 4) for i in range(nt))
            for c2 in range(4):
                b = kb.psum("T")
                pk = "ps%d" % b

                def tr(h, b=b, c2=c2, offs=tuple(offs)):
                    ins = None
                    for q in range(2):
                        c = 2 * c2 + q
                        for i, (r_, n) in enumerate(offs):
                            ins = h.transpose(out=PSB[b][:, q * 512 + r_:q * 512 + r_ + n],
                                              in_=xsb[i % 4][0:n, c * 128:(c + 1) * 128], identity=ident[0:n, 0:n])
                    return ins
                pe(tr, xkeys + ("ident",), (pk,), cost=2 * nt * 75.0)
                for q in range(2):
                    c = 2 * c2 + q
                    evac(hT[:, c, col0:col0 + W], PSB[b][:, q * 512:q * 512 + W],
                         (pk, "pv"), (hTkey,), scale=P(gname, c),
                         eng=(evac_eng[c % len(evac_eng)] if isinstance(evac_eng, tuple) else evac_eng))

        def mm_group(out_ap, pairs, reads, pk, n=None):
            def fn(h):
                ins = None
                nn = len(pairs)
                for i, (l, r) in enumerate(pairs):
                    ins = h.matmul(out_ap, l, r, start=(i == 0), stop=(i == nn - 1))
                return ins
            pe(fn, reads, (pk,), cost=len(pairs) * ((n or kb.W) * 0.44 + 12.0))

        st_kv = ExitStack()
        kT = st_kv.enter_context(nc.sbuf_tensor("kT", [128, 8, 256], BF16))
        vv = st_kv.enter_context(nc.sbuf_tensor("vv", [128, 2, D], BF16))
        st_lc = ExitStack()
        mixreg = st_lc.enter_context(nc.sbuf_tensor("mixreg", [128, 4 * (HALO + TOWN)], BF16))
        fence = st_lc.enter_context(nc.sbuf_tensor("fence", [128, 2], F32))
        mixL = mixreg[:, :].rearrange("p (j n) -> p j n", j=4)
        with ExitStack() as st:
            WK = WS1[:, :].rearrange("p (k n) -> p k n", k=8)
            WV = WS2[:, 0:8192].rearrange("p (k n) -> p k n", k=8)
            load_w(WK, w_kv, 1024, 0, "WS1")
            load_w(WV, w_kv, 1024, 0, "WS2", srccol0=1024)
            memt = mixreg.bitcast(F32)[:, 0:2048].rearrange("p (i n) -> p i n", i=2)
            mT = mixreg[:, 4096:6144].rearrange("p (k n) -> p k n", k=8)
            for i in range(2):
                kb.dma("sp", (lambda h, i=i: h.dma_start(out=memt[:, i, :], in_=memd[i * 128:(i + 1) * 128, :])),
                       (), ("memt%d" % i,), nbytes=524288)
            norm_T([(memt[:, i, :], "memt%d" % i, 128) for i in range(2)], 256, "gm", mT, "mT")
            for dc in range(8):
                b = kb.psum()
                mm_group(PS[b][:, 0:256], [(WK[:, k, dc * 128:(dc + 1) * 128], mT[:, k, :]) for k in range(8)],
                         ("WS1", "mT"), "ps%d" % b)
                evac(kT[:, dc, :], PS[b][:, 0:256], ("ps%d" % b,), ("kT",))
            for mc in range(2):
                for nh in range(2):
                    b = kb.psum()
                    mm_group(PS[b][:, :], [(mT[:, k, mc * 128:(mc + 1) * 128],
                                            WV[:, k, nh * 512:(nh + 1) * 512]) for k in range(8)],
                             ("WS2", "mT"), "ps%d" % b)
                    evac(vv[:, mc, nh * 512:(nh + 1) * 512], PS[b][:, :], ("ps%d" % b,), ("vv",))
            pool(lambda h: h.memset(fence[:], 0.0), (),
                 ("fence", "mT", "memt0", "memt1", "mixL", "WS1") + tuple("W1t%d" % i for i in range(8)), cost=150.0)
        load_w(WIN, w_in, 1536, 512, "WS0b", srccol0=512)

        CINW = 30 + HALO + TOWN
        cinb = WS2[:, 0:4 * CINW].rearrange("p (j n) -> p j n", j=4)
        with ExitStack() as st:
            hTs = [sb("hT%d" % i, [128, 8, 512], BF16, st) for i in range(2)]
            zx = sb("zx", [128, 4, 516], BF16, st)
            lxb = [sb("lxb%d" % i, [128, 512], BF16, st) for i in range(2)]
            hst = sb("hst", [128, 4], F32, st)
            X16 = X.bitcast(BF16)
            xs16 = [X16[:, i // 2, (i % 2) * 1024:(i % 2 + 1) * 1024] for i in range(6)]
            junk2 = X16[:, 5, 0:1024]
            junk3 = X16[:, 5, 1024:2048]
            lcbrow = sb("lcbrow_s", [1, 512], BF16, st)
            onesr = sb("onesr", [1, 512], BF16, st)
            kb.dma("pool", lambda h: h.dma_start(out=lcbrow[:], in_=lcbd), (), ("lcbrow",), nbytes=2048)
            pool(lambda h: h.memset(onesr[:], 1.0), (), ("onesr",), cost=300.0)
            D4 = sb("D4", [128, 16, 128], BF16, st)
            for kk in range(16):
                dve(lambda h, kk=kk: h.tensor_scalar(out=D4[:, kk, :], in0=identf[:], scalar1=P("lcw", kk),
                                                     scalar2=None, op0=ALU.mult), ("identf", "pv"), ("D4",), cost=200.0)
            NXS = 6

            def tmpb(i):
                s = 6 + i // 2
                return X[:, s, (i % 2) * 512:(i % 2) * 512 + 512], "X%dh%d" % (s, i % 2)

            pool(lambda h: h.memset(zx[:], 0.0), (), ("zx0", "zx1", "zx2", "zx3"))
            pool(lambda h: h.memset(hst[:], 0.0), (), ("hst0", "hst1", "hst2", "hst3"))
            pool(lambda h: h.memset(cinb[:, :, 0:30], 0.0), (), ("WS2",))

            tiles = [(512 * j, 512, "pre", j) for j in range(11)]
            tiles.append((5632, 480, "pre", 11))
            tiles.append((6112, 32, "own", 12))
            tiles += [(6144 + 512 * i, 512, "own", 13 + i) for i in range(4)]
            xslot = [0]
            prevW = [None]
            OM3 = [X[:, 12:14, :].rearrange("p s (h n) -> p (s h) n", h=2),
                   X[:, 3:5, :].rearrange("p s (h n) -> p (s h) n", h=2)]
            W1F = WS1.bitcast(F32)
            W1KEYS = tuple("W1t%d" % i for i in range(8))

            def tset(par, j):
                if par == 0:
                    return tmpb(4 + j), tmpb(8 + j), tmpb(12 + j)
                return ((W1F[:, j * 512:(j + 1) * 512], "W1t%d" % j),
                        (W1F[:, (4 + j) * 512:(5 + j) * 512], "W1t%d" % (4 + j)),
                        (X[:, 3 + j // 2, (j % 2) * 512:(j % 2) * 512 + 512], "X%dh%d" % (3 + j // 2, j % 2)))

            for ti, (row0, W, kind, mi) in enumerate(tiles):
                own = kind == "own"
                oc0 = row0 - 6112
                hT = hTs[ti % 2]
                hTk = "hT%d" % (ti % 2)
                kb.W = W
                srcs = []
                for (r, n) in ttiles(W):
                    s = xslot[0] % NXS
                    xslot[0] += 1
                    kb.dma("pool", (lambda h, s=s, r=r, n=n, row0=row0: h.dma_start(
                        out=xs16[s][0:n, :], in_=xall[row0 + r:row0 + r + n, :])), (), ("xs16_%d" % s,),
                        nbytes=n * 4096)
                    srcs.append((xs16[s][0:n, :], "xs16_%d" % s, n))
                norm_T(srcs, W, "g1", hT, hTk, evac_eng="dve", cast2="dve", sq_dve=True)
                for j in range(4):
                    (ti_, tik), (a_, ak), (om_, omk) = tset(ti % 2, j)
                    (tr_, trk) = tmpb(16 if j % 2 == 0 else 20)
                    zk = "zx%d" % j
                    b = kb.psum("Z")
                    pk = "ps%d" % b
                    mm_group(PS[b][:, 0:W],
                             [(WIN[:, k, j * 128:(j + 1) * 128], hT[:, k, 0:W]) for k in range(8)],
                             ("WS0a", hTk), pk)
                    if prevW[0] is not None:
                        pw = prevW[0]
                        dve(lambda h, j=j, pw=pw: h.tensor_copy(out=zx[:, j, 0:3], in_=zx[:, j, pw:pw + 3]),
                            (zk,), (zk,), cost=TINY)
                    dve(lambda h, j=j, b=b, W=W: h.tensor_copy(out=zx[:, j, 3:3 + W], in_=PS[b][:, 0:W]),
                        (pk, zk), (zk,))
                    bc = kb.psum("C")
                    pkc = "ps%d" % bc
                    mm_group(PS[bc][:, 0:W], [(D4[:, k * 4 + j, :], zx[:, j, k:k + W]) for k in range(4)]
                             + [(lcbrow[0:1, j * 128:(j + 1) * 128], onesr[0:1, 0:W])],
                             ("D4", zk, "lcbrow", "onesr"), pkc)
                    lb = lxb[j % 2]
                    lbk = "lxb%d" % (j % 2)
                    dve(lambda h, lb=lb, bc=bc, W=W: h.tensor_copy(out=lb[:, 0:W], in_=PS[bc][:, 0:W]), (pkc,), (lbk,))
                    b1 = kb.psum("G0")
                    b2 = kb.psum("G1")
                    pe(lambda h, b1=b1, j=j, lb=lb, W=W: h.matmul(PS[b1][:, 0:W], BD[:, (j * 2) * 128:(j * 2 + 1) * 128],
                                                                  lb[:, 0:W], start=True, stop=True),
                       ("BD", lbk), ("ps%d" % b1,), cost=W * 0.51 + 15.0)
                    pe(lambda h, b2=b2, j=j, lb=lb, W=W: h.matmul(PS[b2][:, 0:W],
                                                                  BD[:, (j * 2 + 1) * 128:(j * 2 + 2) * 128],
                                                                  lb[:, 0:W], start=True, stop=True),
                       ("BD", lbk), ("ps%d" % b2,), cost=W * 0.51 + 15.0)
                    act(lambda h, b1=b1, W=W, j=j, t=tr_: h.activation(out=t[:, 0:W], in_=PS[b1][:, 0:W],
                                                                       func=AF.Tanh, scale=0.5, bias=SM(S_HBA, j)),
                        ("ps%d" % b1, "sm"), (trk,), tag="s0")
                    act(lambda h, b2=b2, W=W, j=j, t=ti_: h.activation(out=t[:, 0:W], in_=PS[b2][:, 0:W],
                                                                       func=AF.Tanh, scale=0.5, bias=SM(S_HBX, j)),
                        ("ps%d" % b2, "sm"), (tik,), tag="s0")
                    act(lambda h, W=W, j=j, t=tr_, a=a_: h.activation(out=a[:, 0:W], in_=t[:, 0:W], func=AF.Exp,
                                                                      scale=SM(S_HCLA, j), bias=SM(S_HCLA, j)),
                        (trk, "sm"), (ak,), tag="s0")
                    act(lambda h, W=W, j=j, t=tr_, a=om_: h.activation(out=a[:, 0:W], in_=t[:, 0:W], func=AF.Exp,
                                                                       scale=SM(S_CLA, j), bias=SM(S_CLA, j)),
                        (trk, "sm"), (omk,), tag="s0")
                    dve(lambda h, W=W, t=ti_, bc=bc: h.scalar_tensor_tensor(
                        out=t[:, 0:W], in0=t[:, 0:W], scalar=1.0, in1=PS[bc][:, 0:W], op0=ALU.add, op1=ALU.mult),
                        (tik, pkc), (tik,))
                omks = tuple(tset(ti % 2, j)[2][1] for j in range(4))
                om3 = OM3[ti % 2]
                act(lambda h, W=W, mi=mi, om3=om3: h.activation(out=om3[:, :, 0:W], in_=om3[:, :, 0:W], func=AF.Sqrt,
                                                       scale=SM(S_NHM2, mi), bias=SM(S_HM2, mi)),
                    omks + ("sm",), omks, cost=224.0 + 0.833 * 4 * W, tag="sqrt")
                for j in range(4):
                    (ti_, tik), (a_, ak), (om_, omk) = tset(ti % 2, j)
                    (hb_, hbk) = tmpb(17 if j % 2 == 0 else 0)
                    dve(lambda h, W=W, t=ti_, m=om_: h.tensor_tensor(out=t[:, 0:W], in0=t[:, 0:W],
                                                                     in1=m[:, 0:W], op=ALU.mult),
                        (tik, omk), (tik,))
                    dve(lambda h, W=W, a=a_, u=ti_, hb=hb_, j=j: h.tensor_tensor_scan(
                        out=hb[:, 0:W], data0=a[:, 0:W], data1=u[:, 0:W], initial=hst[:, j:j + 1],
                        op0=ALU.mult, op1=ALU.add), (ak, tik, "hst%d" % j), (hbk,), cost=60.0 + 2.08 * W)
                    dve(lambda h, W=W, hb=hb_, j=j: h.tensor_copy(out=hst[:, j:j + 1], in_=hb[:, W - 1:W]),
                        (hbk,), ("hst%d" % j,), cost=TINY)
                    if own:
                        (gs, gsk), (sq, sqk) = tmpb(18), tmpb(19)
                        b = kb.psum("Z")
                        pk = "ps%d" % b
                        mm_group(PS[b][:, 0:W],
                                 [(WIN[:, k, 512 + j * 128:512 + (j + 1) * 128], hT[:, k, 0:W]) for k in range(8)],
                                 ("WS0b", hTk), pk)
                        act(lambda h, b=b, W=W, gs=gs: h.activation(out=gs[:, 0:W], in_=PS[b][:, 0:W],
                                                                    func=AF.Gelu_apprx_tanh), (pk,), (gsk,), tag="gelu")
                        dve(lambda h, W=W, hb=hb_, j=j, oc0=oc0, gs=gs: h.tensor_tensor(
                            out=mixL[:, j, oc0:oc0 + W], in0=gs[:, 0:W], in1=hb[:, 0:W], op=ALU.mult),
                            (gsk, hbk), ("mixL",))
                if own:
                    for j in range(4):
                        (tg, tgk) = tmpb(20 + j % 2)
                        ba_ = kb.psum("C")
                        bb_ = kb.psum("Z")
                        mm_group(PS[ba_][:, 0:W],
                                 [(WIN[:, k, 1024 + j * 128:1024 + (j + 1) * 128], hT[:, k, 0:W]) for k in range(8)],
                                 ("WS0b", hTk), "ps%d" % ba_)
                        mm_group(PS[bb_][:, 0:W],
                                 [(WIN[:, k, 1536 + j * 128:1536 + (j + 1) * 128], hT[:, k, 0:W]) for k in range(8)],
                                 ("WS0b", hTk), "ps%d" % bb_)
                        act(lambda h, bb_=bb_, W=W, tg=tg: h.activation(out=tg[:, 0:W], in_=PS[bb_][:, 0:W],
                                                                        func=AF.Tanh, scale=0.5),
                            ("ps%d" % bb_,), (tgk,), tag="s0")
                        dve(lambda h, ba_=ba_, W=W, tg=tg, j=j, oc0=oc0: h.scalar_tensor_tensor(
                            out=cinb[:, j, 30 + oc0:30 + oc0 + W], in0=tg[:, 0:W], scalar=1.0, in1=PS[ba_][:, 0:W],
                            op0=ALU.add, op1=ALU.mult), (tgk, "ps%d" % ba_), ("WS2",))
                prevW[0] = W
            for k in range(8):
                kb.dma("pool", (lambda h, k=k: h.dma_start(out=WOUT[:, k, :], in_=w_out[k * 128:(k + 1) * 128, :])),
                       (), ("WS1",) + W1KEYS, nbytes=524288)
            kb.barrier()

        own_tiles = [(0, HALO)] + [(HALO + 512 * i, 512) for i in range(4)]

        def xslots(oc0, W):
            if W == HALO:
                return [(0, HALO, 0)]
            base = 1 + (oc0 - HALO) // 128
            return [(base + i, 128, 128 * i) for i in range(4)]

        with ExitStack() as st:
            DIAG = WS0[:, :].rearrange("p (k n) -> p k n", k=128)
            for j in range(4):
                for k in range(31):
                    fnd = (lambda h, k=k, j=j: h.tensor_scalar(out=DIAG[:, k * 4 + j, :], in0=identf[:],
                                                               scalar1=P("ccw", k * 4 + j), scalar2=0.5,
                                                               op0=ALU.mult, op1=ALU.mult))
                    if k % 3 == 2:
                        pool(fnd, ("identf", "pv"), ("dg%d_%d" % (j, k),), cost=400.0)
                    else:
                        dve(fnd, ("identf", "pv"), ("dg%d_%d" % (j, k),), cost=200.0)
            cvs = sb("cvs", [128, 4, 512], F32, st)
            cvb = sb("cvb", [128, 4, 512], BF16, st)
            sqb = sb("sqb", [128, 4, 512], BF16, st)
            mean_s = sb("mean_s", [128, 512], F32, st)
            var_s = sb("var_s", [128, 512], F32, st)
            rstd_s = sb("rstd_s", [128, 512], F32, st)
            xc = [sb("xc%d" % i, [128, 512], F32, st) for i in range(2)]
            mixC = sb("mixC", [128, 4, 512], BF16, st)
            WQ = WS2[:, 0:8192].rearrange("p (k n) -> p k n", k=8)
            WO = WS1[:, :].rearrange("p (k n) -> p k n", k=8)
            WQST = X.bitcast(BF16)[:, 13:17, :].rearrange("p s (h n) -> p (s h) n", h=2)
            stkeys = tuple("X%d" % sl for sl in range(13, 17)) + tuple("X%dh%d" % (sl, hh_) for sl in range(13, 17)
                                                                       for hh_ in range(2))
            for k0 in (0, 4):
                kb.dma("pool", (lambda h, k0=k0: h.dma_start(
                    out=WQST[:, k0:k0 + 4, :],
                    in_=w_q[k0 * 128:(k0 + 4) * 128, :].rearrange("(k p) n -> p k n", p=128))),
                    (), stkeys, nbytes=2097152)
            for c2i, (oc0, W) in enumerate(own_tiles):
                kb.W = W
                for j in range(4):
                    b = kb.psum("A")
                    pk = "ps%d" % b
                    mm_group(PS[b][:, 0:W],
                             [(DIAG[:, k * 4 + j, :], cinb[:, j, oc0 + k:oc0 + k + W]) for k in range(31)],
                             tuple("dg%d_%d" % (j, k) for k in range(31)) + ("WS2",), pk)
                    act(lambda h, b=b, j=j, W=W: h.activation(out=cvs[:, j, 0:W], in_=PS[b][:, 0:W],
                                                              func=AF.Identity, bias=P("ccb", j)),
                        (pk, "pv"), ("cvs%d" % j,))
                    act(lambda h, b=b, j=j, W=W: h.activation(out=sqb[:, j, 0:W], in_=PS[b][:, 0:W],
                                                              func=AF.Square, bias=P("ccb", j)),
                        (pk, "pv"), ("sqb",))
                    dve(lambda h, j=j, W=W: h.tensor_copy(out=cvb[:, j, 0:W], in_=cvs[:, j, 0:W]),
                        ("cvs%d" % j,), ("cvb",))
                if c2i == len(own_tiles) - 1:
                    dve(lambda h: h.tensor_copy(out=WQ, in_=WQST), stkeys, ("WS2",), cost=4500.0)
                bm = kb.psum("M")
                bq = kb.psum("M")
                mm_group(PS[bm][:, 0:W], [(ones_s[:], cvb[:, j, 0:W]) for j in range(4)], ("ones_s", "cvb"),
                         "ps%d" % bm)
                mm_group(PS[bq][:, 0:W], [(ones_s[:], sqb[:, j, 0:W]) for j in range(4)], ("ones_s", "sqb"),
                         "ps%d" % bq)
                dve(lambda h, bm=bm, W=W: h.tensor_copy(out=mean_s[:, 0:W], in_=PS[bm][:, 0:W]),
                    ("ps%d" % bm,), ("mean_s",))
                dve(lambda h, W=W: h.tensor_tensor(out=var_s[:, 0:W], in0=mean_s[:, 0:W], in1=mean_s[:, 0:W],
                                                   op=ALU.mult), ("mean_s",), ("var_s",))
                dve(lambda h, bq=bq, W=W: h.scalar_tensor_tensor(out=var_s[:, 0:W], in0=PS[bq][:, 0:W], scalar=EPS,
                                                                 in1=var_s[:, 0:W], op0=ALU.add, op1=ALU.subtract),
                    ("ps%d" % bq, "var_s"), ("var_s",))
                act(lambda h, W=W: h.activation(out=var_s[:, 0:W], in_=var_s[:, 0:W], func=AF.Sqrt),
                    ("var_s",), ("var_s",), tag="sqrt")
                dve(lambda h, W=W: h.reciprocal(out=rstd_s[:, 0:W], in_=var_s[:, 0:W]), ("var_s",), ("rstd_s",),
                    cost=60.0 + 8.3 * W)
                for j in range(4):
                    x_ = xc[j % 2]
                    xk = "xc%d" % (j % 2)
                    dve(lambda h, j=j, W=W, x_=x_: h.tensor_tensor(out=x_[:, 0:W], in0=cvs[:, j, 0:W],
                                                                   in1=mean_s[:, 0:W], op=ALU.subtract),
                        ("cvs%d" % j, "mean_s"), (xk,))
                    dve(lambda h, W=W, x_=x_: h.tensor_tensor(out=x_[:, 0:W], in0=x_[:, 0:W], in1=rstd_s[:, 0:W],
                                                              op=ALU.mult), (xk, "rstd_s"), (xk,))
                    act(lambda h, j=j, W=W, x_=x_: h.activation(out=mixC[:, j, 0:W], in_=x_[:, 0:W], func=AF.Silu,
                                                                scale=P("lng", j), bias=P("lnb", j)),
                        (xk, "pv"), ("mixC",), tag="silu")
                for (slot, n, co) in xslots(oc0, W):
                    r0 = 6112 + oc0 + co
                    kb.dma("sp", (lambda h, slot=slot, n=n, r0=r0: h.dma_start(
                        out=X[0:n, slot, :], in_=xall[r0:r0 + n, :])), (),
                        ("X%d" % slot, "X%dh0" % slot, "X%dh1" % slot), nbytes=n * 4096)
                    for nh in range(2):
                        b = kb.psum("O")
                        pk = "ps%d" % b
                        pairs = [(mixL[:, k, oc0 + co:oc0 + co + n], WOUT[:, k, nh * 512:(nh + 1) * 512])
                                 for k in range(4)]
                        pairs += [(mixC[:, k, co:co + n], WOUT[:, 4 + k, nh * 512:(nh + 1) * 512]) for k in range(4)]
                        mm_group(PS[b][0:n, :], pairs, ("mixL", "mixC", "WS1"), pk, n=512)
                        dve(lambda h, b=b, n=n, slot=slot, nh=nh: h.tensor_tensor(
                            out=X[0:n, slot, nh * 512:(nh + 1) * 512], in0=PS[b][0:n, :],
                            in1=X[0:n, slot, nh * 512:(nh + 1) * 512], op=ALU.add),
                            (pk, "X%d" % slot), ("X%d" % slot,))
            load_w(WO, w_o, 1024, 0, "WS1")
            kb.barrier()
        st_lc.close()

        def write_out_debug():
            toks = []
            for i in range(16):
                toks.append(kb.dma("sp", (lambda h, i=i: h.dma_start(out=yd[i * 128:(i + 1) * 128, :],
                                                                    in_=X[:, 1 + i, :])), ("X%d" % (1 + i),), (), nbytes=524288))

        def ffn_views(gi):
            if gi % 2 == 0:
                return (WS0[:, 0:8192].rearrange("p (k n) -> p k n", k=8),
                        WS0[:, 8192:12288].rearrange("p (k n) -> p k n", k=4), "WS0a", "WS0a")
            return (WS1[:, :].rearrange("p (k n) -> p k n", k=8),
                    WS2[:, 0:4096].rearrange("p (k n) -> p k n", k=4), "WS1", "WS2")

        def ffn_load(gi):
            WU, WD, wk, wkd = ffn_views(gi)
            load_w(WU, w_up, 512, 0, wk, srccol0=gi * 512)
            load_w(WU, w_up, 512, 512, wk, srccol0=3072 + gi * 512)
            load_w(WD, w_down, 1024, 0, wkd, nk=4, row0=gi * 512)

        if DEBUG_STAGE == "C2":
            write_out_debug()
            st_kv.close()
        else:
            with ExitStack() as st:
                WQ = WS2[:, 0:8192].rearrange("p (k n) -> p k n", k=8)
                WO = WS1[:, :].rearrange("p (k n) -> p k n", k=8)
                ffn_load(0)
                h2Ts = [sb("h2T%d" % i, [128, 8, 512], BF16, st) for i in range(2)]
                qT = sb("qT", [128, 8, 512], BF16, st)
                pT = [sb("pT%d" % i, [128, 2, 512], BF16, st) for i in range(2)]
                rrec = [sb("rrec%d" % i, [128, 512], F32, st) for i in range(2)]
                oTs = [sb("oT%d" % i, [128, 8, 512], BF16, st) for i in range(2)]
                for ai, (oc0, W) in enumerate(own_tiles):
                    sl = xslots(oc0, W)
                    h2T = h2Ts[ai % 2]
                    h2k = "h2T%d" % (ai % 2)
                    oT = oTs[ai % 2]
                    oTk = "oT%d" % (ai % 2)
                    kb.W = W
                    norm_T([(X[0:n, s, :], "X%d" % s, n) for (s, n, co) in sl], W, "g2", h2T, h2k)
                    for dc in range(8):
                        b = kb.psum("Z")
                        mm_group(PS[b][:, 0:W], [(WQ[:, k, dc * 128:(dc + 1) * 128], h2T[:, k, 0:W]) for k in range(8)],
                                 ("WS2", h2k), "ps%d" % b)
                        evac(qT[:, dc, 0:W], PS[b][:, 0:W], ("ps%d" % b,), ("qT",))
                    for hh in range(4):
                        p_ = pT[hh % 2]
                        pkk = "pT%d" % (hh % 2)
                        rr = rrec[hh % 2]
                        rk = "rrec%d" % (hh % 2)
                        for mc in range(2):
                            b = kb.psum("C")
                            mm_group(PS[b][:, 0:W],
                                     [(kT[:, 2 * hh + dcc, mc * 128:(mc + 1) * 128], qT[:, 2 * hh + dcc, 0:W])
                                      for dcc in range(2)], ("kT", "qT"), "ps%d" % b)
                            act(lambda h, b=b, W=W, p_=p_, mc=mc: h.activation(out=p_[:, mc, 0:W], in_=PS[b][:, 0:W],
                                                                               func=AF.Exp, scale=1.0 / 16.0),
                                ("ps%d" % b,), (pkk,), tag="lnexp")
                        b = kb.psum("G0")
                        mm_group(PS[b][:, 0:W], [(ones1[:], p_[:, mc, 0:W]) for mc in range(2)], ("ones1", pkk),
                                 "ps%d" % b)
                        act(lambda h, b=b, W=W, rr=rr: h.activation(out=rr[:, 0:W], in_=PS[b][:, 0:W], func=AF.Ln),
                            ("ps%d" % b,), (rk,), tag="lnexp")
                        act(lambda h, W=W, rr=rr: h.activation(out=rr[:, 0:W], in_=rr[:, 0:W], func=AF.Exp, scale=-1.0),
                            (rk,), (rk,), tag="lnexp")
                        for dcc in range(2):
                            b = kb.psum("G1")
                            mm_group(PS[b][:, 0:W],
                                     [(vv[:, mc, hh * 256 + dcc * 128:hh * 256 + (dcc + 1) * 128], p_[:, mc, 0:W])
                                      for mc in range(2)], ("vv", pkk), "ps%d" % b)
                            dve(lambda h, b=b, W=W, rr=rr, hh=hh, dcc=dcc, oT=oT: h.tensor_tensor(
                                out=oT[:, 2 * hh + dcc, 0:W], in0=PS[b][:, 0:W], in1=rr[:, 0:W], op=ALU.mult),
                                ("ps%d" % b, rk), (oTk,))
                    for (slot, n, co) in sl:
                        for nh in range(2):
                            b = kb.psum("O")
                            mm_group(PS[b][0:n, :], [(oT[:, k, co:co + n], WO[:, k, nh * 512:(nh + 1) * 512])
                                                     for k in range(8)], (oTk, "WS1"), "ps%d" % b, n=512)
                            dve(lambda h, b=b, n=n, slot=slot, nh=nh: h.tensor_tensor(
                                out=X[0:n, slot, nh * 512:(nh + 1) * 512], in0=PS[b][0:n, :],
                                in1=X[0:n, slot, nh * 512:(nh + 1) * 512], op=ALU.add),
                                ("ps%d" % b, "X%d" % slot), ("X%d" % slot,))
                kb.barrier()
            st_kv.close()

            if DEBUG_STAGE == "A":
                write_out_debug()
            else:
                with ExitStack() as st:
                    h3T = sb("h3T", [128, 8, HALO + TOWN], BF16, st)
                    graw = [sb("graw%d" % i, [128, 516], F32, st) for i in range(2)]
                    acc = [sb("acc%d" % i, [128, 512], F32, st) for i in range(2)]
                    hid = [sb("hid%d" % i, [128, 4, 512], BF16, st) for i in range(2)]
                    carry = sb("carry", [128, 8], F32, st)
                    gf = sb("gf", [128, D], F32, st)
                    kb.dma("sp", lambda h: h.dma_start(out=gf[:], in_=gfd), (), ("gf",), nbytes=524288)
                    for (oc0, W) in own_tiles:
                        sl = xslots(oc0, W)
                        norm_T([(X[0:n, s, :], "X%d" % s, n) for (s, n, co) in sl], W, "g3", h3T, "h3T", col0=oc0)
                    for gi in range(6):
                        WU, WD, wk, wkd = ffn_views(gi)
                        if gi > 0:
                            ffn_load(gi)
                        for wi, (oc0, W) in enumerate(own_tiles):
                            halo = W == HALO
                            kb.W = W
                            hd = hid[wi % 2]
                            hk = "hid%d" % (wi % 2)
                            bus = {}

                            def stA(c, gi=gi, W=W, oc0=oc0, halo=halo, WU=WU, wk=wk):
                                hc = gi * 4 + c
                                g_ = graw[c % 2]
                                gk = "graw%d" % (c % 2)
                                a_ = acc[c % 2]
                                ak = "acc%d" % (c % 2)
                                bg = kb.psum("FG")
                                mm_group(PS[bg][:, 0:W],
                                         [(WU[:, k, c * 128:(c + 1) * 128], h3T[:, k, oc0:oc0 + W]) for k in range(8)],
                                         (wk, "h3T"), "ps%d" % bg)
                                act(lambda h, bg=bg, W=W, g_=g_: h.activation(out=g_[:, 2:2 + W], in_=PS[bg][:, 0:W],
                                                                              func=AF.Copy),
                                    ("ps%d" % bg,), (gk,))
                                if halo:
                                    dve(lambda h, g_=g_, c=c: h.tensor_scalar(
                                        out=carry[:, 2 * c:2 * c + 2], in0=g_[:, HALO:HALO + 2],
                                        scalar1=P("mask", 12), scalar2=None, op0=ALU.mult), (gk, "pv"), ("carry",),
                                        cost=TINY)
                                    return
                                dve(lambda h, g_=g_, c=c: h.tensor_copy(out=g_[:, 0:2], in_=carry[:, 2 * c:2 * c + 2]),
                                    ("carry",), (gk,), cost=TINY)
                                dve(lambda h, g_=g_, c=c, W=W: h.tensor_copy(out=carry[:, 2 * c:2 * c + 2],
                                                                             in_=g_[:, W:W + 2]), (gk,), ("carry",), cost=TINY)
                                act(lambda h, g_=g_, a_=a_, W=W, hc=hc: h.activation(
                                    out=a_[:, 0:W], in_=g_[:, 2:2 + W], func=AF.Identity,
                                    scale=P("fcw", 2 * 24 + hc), bias=P("fcb", hc)), (gk, "pv"), (ak,))
                                for k in range(2):
                                    dve(lambda h, g_=g_, a_=a_, W=W, hc=hc, k=k: h.scalar_tensor_tensor(
                                        out=a_[:, 0:W], in0=g_[:, k:k + W], scalar=P("fcw", k * 24 + hc),
                                        in1=a_[:, 0:W], op0=ALU.mult, op1=ALU.add), (gk, ak, "pv"), (ak,))
                                act(lambda h, a_=a_, W=W: h.activation(out=a_[:, 0:W], in_=a_[:, 0:W],
                                                                       func=AF.Gelu_apprx_tanh), (ak,), (ak,), tag="gelu")

                            def stB(c, W=W, oc0=oc0, WU=WU, wk=wk, hd=hd, hk=hk):
                                a_ = acc[c % 2]
                                ak = "acc%d" % (c % 2)
                                bu = kb.psum("FU")
                                mm_group(PS[bu][:, 0:W],
                                         [(WU[:, k, 512 + c * 128:512 + (c + 1) * 128], h3T[:, k, oc0:oc0 + W])
                                          for k in range(8)], (wk, "h3T"), "ps%d" % bu)
                                dve(lambda h, a_=a_, W=W, bu=bu, hd=hd, c=c: h.tensor_tensor(
                                    out=hd[:, c, 0:W], in0=a_[:, 0:W], in1=PS[bu][:, 0:W], op=ALU.mult),
                                    (ak, "ps%d" % bu), (hk,))

                            if halo:
                                for c in range(4):
                                    stA(c)
                            else:
                                stA(0)
                                stA(1)
                                stB(0)
                                stA(2)
                                stB(1)
                                stA(3)
                                stB(2)
                                stB(3)
                            if halo:
                                continue
                            for (slot, n, co) in xslots(oc0, W):
                                for nh in range(2):
                                    b = kb.psum("FD")
                                    mm_group(PS[b][0:n, :],
                                             [(hd[:, c, co:co + n], WD[:, c, nh * 512:(nh + 1) * 512])
                                              for c in range(4)], (hk, wkd), "ps%d" % b, n=512)
                                    dve(lambda h, b=b, n=n, slot=slot, nh=nh: h.tensor_tensor(
                                        out=X[0:n, slot, nh * 512:(nh + 1) * 512], in0=PS[b][0:n, :],
                                        in1=X[0:n, slot, nh * 512:(nh + 1) * 512], op=ALU.add),
                                        ("ps%d" % b, "X%d" % slot), ("X%d" % slot,))
                    toks = []
                    for i in range(16):
                        slot = 1 + i
                        f_ = X[:, slot, :]
                        fk = "X%d" % slot
                        c4 = i % 4
                        act(lambda h, slot=slot, c4=c4: h.activation(out=junk[:, :], in_=X[:, slot, :], func=AF.Square,
                                                                     accum_out=ss[:, c4:c4 + 1]),
                            ("X%d" % slot,), ("ss%d" % c4, "junk"), cost=1170.0)
                        pool(lambda h, c4=c4: h.tensor_scalar(out=rs[:, c4:c4 + 1], in0=ss[:, c4:c4 + 1],
                                                              scalar1=1.0 / D, scalar2=EPS, op0=ALU.mult, op1=ALU.add),
                             ("ss%d" % c4,), ("rs",), cost=200.0)
                        pool(lambda h, c4=c4: h.tensor_tensor(out=rs[:, c4:c4 + 1], in0=rs[:, c4:c4 + 1],
                                                              in1=neghT[:, 0:1], op=ALU.pow), ("rs", "neghT"), ("rs",),
                             cost=400.0)
                        dve(lambda h, slot=slot, c4=c4, f_=f_: h.scalar_tensor_tensor(
                            out=f_, in0=f_, scalar=rs[:, c4:c4 + 1], in1=gf[:, :],
                            op0=ALU.mult, op1=ALU.mult), (fk, "rs", "gf"), (fk,), cost=1125.0)
                        toks.append(kb.dma("sp", (lambda h, i=i, f_=f_: h.dma_start(
                            out=yd[i * 128:(i + 1) * 128, :], in_=f_)), (fk,), (), nbytes=524288))

        kb.finalize()
        with nc.Block() as block:
            @block.tensor
            def _(h):
                kb.replay(h, "pe")

            @block.scalar
            def _(h):
                kb.replay(h, "act")

            @block.vector
            def _(h):
                kb.replay(h, "dve")

            @block.gpsimd
            def _(h):
                kb.replay(h, "pool")

            @block.sync
            def _(h):
                kb.replay(h, "sp")
    return nc


def _pack_params(inp, core):
    k = core % 4
    pvv = np.zeros((128, NPV), np.float32)

    def put(name, arr2d):
        o = PCOLS[name]
        pvv[:, o:o + arr2d.shape[0]] = arr2d.T

    ch = lambda v, n: np.asarray(v, np.float32).reshape(n, 128)
    put("g1", ch(inp["mix_norm_g"][0], 8))
    put("lcw", np.asarray(inp["lru_conv_w"][0], np.float32).reshape(4 * 4, 128))
    put("lcb", ch(inp["lru_conv_b"][0], 4))
    put("ba", ch(inp["lru_b_a"][0], 4))
    put("bx", ch(inp["lru_b_x"][0], 4))
    put("lam", ch(inp["lru_lambda"][0], 4))
    put("ccw", np.asarray(inp["conf_conv_w"][0], np.float32).reshape(31 * 4, 128))
    put("ccb", ch(inp["conf_conv_b"][0], 4))
    put("lng", ch(inp["conf_ln_g"][0], 4))
    put("lnb", ch(inp["conf_ln_b"][0], 4))
    put("g2", ch(inp["xa_norm_g"][0], 8))
    put("gm", ch(inp["mem_norm_g"][0], 8))
    put("g3", ch(inp["ffn_norm_g"][0], 8))
    put("fcw", np.asarray(inp["ffn_conv_w"][0], np.float32).reshape(3 * 24, 128))
    put("fcb", ch(inp["ffn_conv_b"][0], 24))
    lo = 2048 * k - 6144
    m = np.ones(17, np.float32)
    for j in range(12):
        m[j] = 1.0 if lo + 512 * j >= 0 else 0.0
    m[12] = m[11]
    pvv[:, PCOLS["mask"]:PCOLS["mask"] + 17] = m[None, :]
    return pvv


def kernel(**inputs):
    inp = {k: np.asarray(v) for k, v in inputs.items()}
    x = inp["x"].astype(np.float32, copy=False)
    mem = inp["mem"].astype(np.float32, copy=False)
    nc = build_nc()
    gfin = np.ascontiguousarray(np.broadcast_to(inp["final_norm_g"].astype(np.float32)[None, :], (128, D)))
    bd = np.zeros((128, 4, 2, 128), np.float32)
    for j in range(4):
        for s, nm in enumerate(("lru_w_a", "lru_w_x")):
            w = inp[nm][0]
            bd[0:64, j, s, 0:64] = w[2 * j]
            bd[64:128, j, s, 64:128] = w[2 * j + 1]
    bd = bd.reshape(128, 1024)
    shared = dict(gfin=gfin, bd=bd, lcbrow=np.ascontiguousarray(inp["lru_conv_b"][0].astype(np.float32).reshape(1, 512)), w_in=np.ascontiguousarray(inp["w_in"][0]), w_out=np.ascontiguousarray(inp["w_out"][0]),
                  w_q=np.ascontiguousarray(inp["w_q"][0]), w_kv=np.ascontiguousarray(inp["w_kv"][0]),
                  w_o=np.ascontiguousarray(inp["w_o"][0]), w_up=np.ascontiguousarray(inp["w_up"][0]),
                  w_down=np.ascontiguousarray(inp["w_down"][0]))
    in_maps = []
    for c in range(8):
        b, k = c // 4, c % 4
        t0 = 2048 * k
        lo = t0 - 6144
        xa = np.zeros((NROWS, D), np.float32)
        s = max(0, lo)
        xa[s - lo:] = x[b, s:t0 + 2048]
        d = dict(shared)
        d["xall"] = xa
        d["mem"] = np.ascontiguousarray(mem[b])
        d["pv"] = _pack_params(inp, c)
        in_maps.append(d)
    res = run_bass_kernel_spmd(nc, in_maps, core_ids=list(range(8)))
    out = np.zeros((2, SEQ, D), np.float32)
    for c in range(8):
        b, k = c // 4, c % 4
        out[b, 2048 * k:2048 * (k + 1)] = res.results[c]["y"]
    return out
```
